# Optimizing a Trainium2 kernel written in Bass

```python
import jax, jax.numpy as jnp
from jax import lax
import numpy as np

D_MODEL = 1024
BATCH = 16
SEQ = 4096
DEPTH = 1
DEC_BATCH = 16
DEC_SEQ = 16
PAST_LEN = 4096

CHUNK = 64
SGU_CHUNK = 128
SGU_GROUPS = 8
SGU_WIDTH = D_MODEL
SGU_GROUP_DIM = SGU_WIDTH // SGU_GROUPS
N_HEADS = 16
KV_HEADS = 4
HEAD_DIM = 64
Q_GROUP = N_HEADS // KV_HEADS
SWA_WINDOW = 128
ROT_DIM = HEAD_DIM // 4
ROPE_THETA = 500000.0
D_FF = 4 * D_MODEL
NORM_EPS = 1e-6
IN_SPLITS = (SGU_WIDTH, SGU_WIDTH, N_HEADS * HEAD_DIM, KV_HEADS * HEAD_DIM, KV_HEADS * HEAD_DIM, D_MODEL, D_MODEL)
IN_WIDTH = sum(IN_SPLITS)

kernel_name = "streaming_sgu_swa_parallel_gated_encoder_step"


def _rmsnorm(x, g):
    xf = x.astype(jnp.float32)
    y = xf * lax.rsqrt(jnp.mean(xf * xf, axis=-1, keepdims=True) + NORM_EPS)
    return (y * g.astype(jnp.float32)).astype(x.dtype)


def _layernorm(x, g, b):
    xf = x.astype(jnp.float32)
    xc = xf - jnp.mean(xf, axis=-1, keepdims=True)
    y = xc * lax.rsqrt(jnp.mean(xc * xc, axis=-1, keepdims=True) + NORM_EPS)
    return (y * g.astype(jnp.float32) + b.astype(jnp.float32)).astype(x.dtype)


def _rope(x, pos):
    half = ROT_DIM // 2
    inv = ROPE_THETA ** (-jnp.arange(half, dtype=jnp.float32) * 2.0 / ROT_DIM)
    ang = pos[:, None] * inv[None, :]
    cos = jnp.cos(ang)[:, None, :]
    sin = jnp.sin(ang)[:, None, :]
    xr = x[..., :ROT_DIM].astype(jnp.float32)
    x1, x2 = xr[..., :half], xr[..., half:]
    rot = jnp.concatenate([x1 * cos - x2 * sin, x2 * cos + x1 * sin], axis=-1).astype(x.dtype)
    return jnp.concatenate([rot, x[..., ROT_DIM:]], axis=-1)


def _split_in(z):
    offs = np.cumsum(IN_SPLITS)[:-1].tolist()
    return jnp.split(z, offs, axis=-1)


def _sgu_mask():
    i = jnp.arange(SGU_CHUNK)
    return (i[:, None] // CHUNK) >= (i[None, :] // CHUNK)


def _sgu_prompt(u, v, ln_g, ln_b, w_s, b_s):
    B, S = u.shape[:2]
    vn = _layernorm(jax.nn.gelu(v, approximate=False), ln_g, ln_b)
    w = jnp.where(_sgu_mask()[None], w_s, 0)
    vb = vn.reshape(B, S // SGU_CHUNK, SGU_CHUNK, SGU_GROUPS, SGU_GROUP_DIM)
    s = jnp.einsum('gij,bnjgc->bnigc', w, vb) + b_s.T[:, :, None]
    return jax.nn.gelu(u, approximate=False) * s.reshape(B, S, SGU_WIDTH)


def _sgu_sample(u, v, ln_g, ln_b, w_s, b_s):
    B, T = u.shape[:2]
    vn = _layernorm(jax.nn.gelu(v, approximate=False), ln_g, ln_b)
    w = jnp.where(_sgu_mask()[None], w_s, 0)[:, :T, :T]
    vb = vn.reshape(B, T, SGU_GROUPS, SGU_GROUP_DIM)
    s = jnp.einsum('gij,bjgc->bigc', w, vb) + b_s[:, :T].T[:, :, None]
    return jax.nn.gelu(u, approximate=False) * s.reshape(B, T, SGU_WIDTH), vn


def _sink_attention(q, k, v, sinks, mask):
    s = jnp.einsum('...qhgd,...khd->...hgqk', q, k, preferred_element_type=jnp.float32) * (HEAD_DIM ** -0.5)
    if mask is not None:
        s = jnp.where(mask, s, -1e30)
    sink = sinks.astype(jnp.float32).reshape(KV_HEADS, Q_GROUP)[:, :, None, None]
    m = jnp.maximum(jnp.max(s, axis=-1, keepdims=True), sink)
    e = jnp.exp(s - m)
    p = e / (jnp.sum(e, axis=-1, keepdims=True) + jnp.exp(sink - m))
    return jnp.einsum('...hgqk,...khd->...qhgd', p.astype(v.dtype), v)


def _swa_prompt(q, k, v, sinks):
    B, S = q.shape[:2]
    n = S // CHUNK
    nb = SWA_WINDOW // CHUNK
    pos = jnp.arange(S, dtype=jnp.float32)
    q = _rope(q, pos)
    k = _rope(k, pos)
    pad = ((0, 0), (SWA_WINDOW, 0), (0, 0), (0, 0))
    kp = jnp.pad(k, pad).reshape(B, n + nb, CHUNK, KV_HEADS, HEAD_DIM)
    vp = jnp.pad(v, pad).reshape(B, n + nb, CHUNK, KV_HEADS, HEAD_DIM)
    kb = jnp.concatenate([kp[:, i:i + n] for i in range(nb + 1)], axis=2)
    vb = jnp.concatenate([vp[:, i:i + n] for i in range(nb + 1)], axis=2)
    qb = q.reshape(B, n, CHUNK, KV_HEADS, Q_GROUP, HEAD_DIM)
    key_pos = jnp.arange(n)[:, None] * CHUNK - SWA_WINDOW + jnp.arange(SWA_WINDOW + CHUNK)[None, :]
    mask = (key_pos >= 0)[:, None, None, None, :]
    o = _sink_attention(qb, kb, vb, sinks, mask).reshape(B, S, N_HEADS * HEAD_DIM)
    return o, k[:, S - SWA_WINDOW:], v[:, S - SWA_WINDOW:]


def _swa_sample(q, k, v, cache_k, cache_v, sinks):
    B, T = q.shape[:2]
    pos = PAST_LEN + jnp.arange(T, dtype=jnp.float32)
    q = _rope(q, pos)
    k = _rope(k, pos)
    ka = jnp.concatenate([cache_k, k], axis=1)
    va = jnp.concatenate([cache_v, v], axis=1)
    qb = q.reshape(B, T, KV_HEADS, Q_GROUP, HEAD_DIM)
    o = _sink_attention(qb, ka, va, sinks, None).reshape(B, T, N_HEADS * HEAD_DIM)
    return o, k, v


def _merge(a, b, ga, gb, w_branch_a, w_branch_b, w_out):
    m = jax.nn.sigmoid(ga) * (a @ w_branch_a) + jax.nn.sigmoid(gb) * (b @ w_branch_b)
    return m @ w_out


def _ffn(h, g_pre, g_post, w_ff1, w_ff2):
    z = jnp.square(jax.nn.relu(_rmsnorm(h, g_pre) @ w_ff1)) @ w_ff2
    return h + _rmsnorm(z, g_post)


def _layer(h_p, h_s, ck, cv, w_in, sgu_ln_g, sgu_ln_b, sgu_w, sgu_b, attn_sinks, w_branch_a, w_branch_b,
           w_out, g_mix_pre, g_mix_post, g_ffn_pre, g_ffn_post, w_ff1, w_ff2):
    B, S = h_p.shape[:2]
    u, v, q, k, vv, ga, gb = _split_in(_rmsnorm(h_p, g_mix_pre) @ w_in)
    a = _sgu_prompt(u, v, sgu_ln_g, sgu_ln_b, sgu_w, sgu_b)
    b, k_last, v_last = _swa_prompt(q.reshape(B, S, N_HEADS, HEAD_DIM), k.reshape(B, S, KV_HEADS, HEAD_DIM),
                                    vv.reshape(B, S, KV_HEADS, HEAD_DIM), attn_sinks)
    h_p = h_p + _rmsnorm(_merge(a, b, ga, gb, w_branch_a, w_branch_b, w_out), g_mix_post)
    h_p = _ffn(h_p, g_ffn_pre, g_ffn_post, w_ff1, w_ff2)
    Bs, T = h_s.shape[:2]
    u, v, q, k, vv, ga, gb = _split_in(_rmsnorm(h_s, g_mix_pre) @ w_in)
    a, v_sgu = _sgu_sample(u, v, sgu_ln_g, sgu_ln_b, sgu_w, sgu_b)
    b, k_new, v_new = _swa_sample(q.reshape(Bs, T, N_HEADS, HEAD_DIM), k.reshape(Bs, T, KV_HEADS, HEAD_DIM),
                                  vv.reshape(Bs, T, KV_HEADS, HEAD_DIM), ck, cv, attn_sinks)
    h_s = h_s + _rmsnorm(_merge(a, b, ga, gb, w_branch_a, w_branch_b, w_out), g_mix_post)
    h_s = _ffn(h_s, g_ffn_pre, g_ffn_post, w_ff1, w_ff2)
    return h_p, h_s, k_last, v_last, k_new, v_new, v_sgu


def setup_inputs(seed: int = 0) -> dict:
    key = jax.random.key(seed)
    ks = jax.random.split(key, 20)
    f32 = jnp.float32

    def nrm(k, shape, scale):
        return jax.random.normal(k, shape, f32) * scale

    return {
        "x_prompt": nrm(ks[0], (BATCH, SEQ, D_MODEL), 1.0),
        "x_sample": nrm(ks[1], (DEC_BATCH, DEC_SEQ, D_MODEL), 1.0),
        "cache_swa_k": nrm(ks[2], (DEPTH, DEC_BATCH, SWA_WINDOW, KV_HEADS, HEAD_DIM), 1.0),
        "cache_swa_v": nrm(ks[3], (DEPTH, DEC_BATCH, SWA_WINDOW, KV_HEADS, HEAD_DIM), 1.0),
        "w_in": nrm(ks[4], (DEPTH, D_MODEL, IN_WIDTH), D_MODEL ** -0.5),
        "sgu_ln_g": 1.0 + nrm(ks[5], (DEPTH, SGU_WIDTH), 0.02),
        "sgu_ln_b": nrm(ks[6], (DEPTH, SGU_WIDTH), 0.02),
        "sgu_w": nrm(ks[7], (DEPTH, SGU_GROUPS, SGU_CHUNK, SGU_CHUNK), SGU_CHUNK ** -0.5),
        "sgu_b": 1.0 + nrm(ks[8], (DEPTH, SGU_GROUPS, SGU_CHUNK), 0.02),
        "attn_sinks": nrm(ks[9], (DEPTH, N_HEADS), 0.5),
        "w_branch_a": nrm(ks[10], (DEPTH, SGU_WIDTH, D_MODEL), SGU_WIDTH ** -0.5),
        "w_branch_b": nrm(ks[11], (DEPTH, N_HEADS * HEAD_DIM, D_MODEL), (N_HEADS * HEAD_DIM) ** -0.5),
        "w_out": nrm(ks[12], (DEPTH, D_MODEL, D_MODEL), D_MODEL ** -0.5),
        "g_mix_pre": 1.0 + nrm(ks[13], (DEPTH, D_MODEL), 0.02),
        "g_mix_post": 1.0 + nrm(ks[14], (DEPTH, D_MODEL), 0.02),
        "g_ffn_pre": 1.0 + nrm(ks[15], (DEPTH, D_MODEL), 0.02),
        "g_ffn_post": 1.0 + nrm(ks[16], (DEPTH, D_MODEL), 0.02),
        "w_ff1": nrm(ks[17], (DEPTH, D_MODEL, D_FF), D_MODEL ** -0.5),
        "w_ff2": nrm(ks[18], (DEPTH, D_FF, D_MODEL), D_FF ** -0.5),
    }


def reference(x_prompt, x_sample, cache_swa_k, cache_swa_v, w_in, sgu_ln_g, sgu_ln_b, sgu_w, sgu_b, attn_sinks,
              w_branch_a, w_branch_b, w_out, g_mix_pre, g_mix_post, g_ffn_pre, g_ffn_post, w_ff1, w_ff2):
    h_p, h_s = x_prompt, x_sample
    kp_l, vp_l, ks_l, vs_l, us_l = [], [], [], [], []
    for l in range(DEPTH):
        h_p, h_s, kp, vp, ksn, vsn, usn = _layer(
            h_p, h_s, cache_swa_k[l], cache_swa_v[l], w_in[l], sgu_ln_g[l], sgu_ln_b[l], sgu_w[l], sgu_b[l],
            attn_sinks[l], w_branch_a[l], w_branch_b[l], w_out[l], g_mix_pre[l], g_mix_post[l], g_ffn_pre[l],
            g_ffn_post[l], w_ff1[l], w_ff2[l])
        kp_l.append(kp)
        vp_l.append(vp)
        ks_l.append(ksn)
        vs_l.append(vsn)
        us_l.append(usn)
    return (h_p, h_s, jnp.stack(kp_l), jnp.stack(vp_l), jnp.stack(ks_l), jnp.stack(vs_l), jnp.stack(us_l))
```

```python
import numpy as np
import concourse.bass as bass
import concourse.mybir as mybir
from concourse.bass_utils import run_bass_kernel_spmd

F32 = mybir.dt.float32
BF16 = mybir.dt.bfloat16
AF = mybir.ActivationFunctionType
ALU = mybir.AluOpType

NCORES = 8
SEQ = 4096
D = 1024
EPS = 1e-6
NSLOT = 5
NXS = 8
NKS = 6
NBLK = 33
SAME_ENGINE_WAW = False


class Res:
    __slots__ = ("w", "r", "name")

    def __init__(self, name=""):
        self.w = {}
        self.r = {}
        self.name = name


class Sched:
    ENG = ("pe", "act", "dve", "pool", "sp")

    def __init__(self, nc):
        self.nc = nc
        self.prog = {e: [] for e in self.ENG}
        self.abs = {e: [] for e in self.ENG}
        self.sem = {}
        self.cnt = {}
        self.waited = {e: {} for e in self.ENG}
        for e in ("pe", "act", "dve", "pool"):
            self.newsem(e)

    def newsem(self, key):
        self.sem[key] = self.nc.alloc_semaphore("s_" + key)
        self.cnt[key] = 0

    def _deps(self, eng, reads, writes):
        deps = {}
        for r in reads:
            for k, v in r.w.items():
                if k == eng and eng == "pe":
                    continue
                if deps.get(k, 0) < v:
                    deps[k] = v
        for w in writes:
            for k, v in w.w.items():
                if k == eng and (eng == "pe" or not SAME_ENGINE_WAW):
                    continue
                if deps.get(k, 0) < v:
                    deps[k] = v
            for k, v in w.r.items():
                if k == eng and (eng == "pe" or not SAME_ENGINE_WAW):
                    continue
                if deps.get(k, 0) < v:
                    deps[k] = v
        return deps

    def _emit_waits(self, eng, deps):
        wd = self.waited[eng]
        for k, v in deps.items():
            if wd.get(k, 0) < v:
                wd[k] = v
                h = self.sem[k]
                self.prog[eng].append(lambda e, h=h, v=v: e.wait_ge(h, v))
                self.abs[eng].append(("wait", k, v))

    def op(self, eng, fn, reads=(), writes=(), inc=True):
        deps = self._deps(eng, reads, writes)
        self._emit_waits(eng, deps)
        if inc:
            self.cnt[eng] += 1
            v = self.cnt[eng]
            h = self.sem[eng]
            self.prog[eng].append(lambda e, fn=fn, h=h: fn(e).then_inc(h, 1))
            self.abs[eng].append(("inc", eng, 1))
        else:
            v = self.cnt[eng] + 1
            self.prog[eng].append(lambda e, fn=fn: fn(e))
        for w in writes:
            w.w = {eng: v}
            w.r = {}
        for r in reads:
            if r.r.get(eng, 0) < v:
                r.r[eng] = v

    def dma(self, qeng, semkey, fn, reads=(), writes=()):
        if semkey not in self.sem:
            self.newsem(semkey)
        deps = self._deps(None, reads, writes)
        self._emit_waits(qeng, deps)
        self.cnt[semkey] += 16
        v = self.cnt[semkey]
        h = self.sem[semkey]
        self.prog[qeng].append(lambda e, fn=fn, h=h: fn(e).then_inc(h, 16))
        self.abs[qeng].append(("inc", semkey, 16))
        for w in writes:
            w.w = {semkey: v}
            w.r = {}
        for r in reads:
            if r.r.get(semkey, 0) < v:
                r.r[semkey] = v

    def check_deadlock(self):
        val = {k: 0 for k in self.sem}
        pc = {e: 0 for e in self.ENG}
        progress = True
        while progress:
            progress = False
            for e in self.ENG:
                lst = self.abs[e]
                while pc[e] < len(lst):
                    kind, k, v = lst[pc[e]]
                    if kind == "wait":
                        if val[k] >= v:
                            pc[e] += 1
                            progress = True
                        else:
                            break
                    else:
                        val[k] += v
                        pc[e] += 1
                        progress = True
        stuck = {e: (pc[e], len(self.abs[e]), self.abs[e][pc[e]] if pc[e] < len(self.abs[e]) else None, val.get(self.abs[e][pc[e]][1]) if pc[e] < len(self.abs[e]) else None)
                 for e in self.ENG if pc[e] < len(self.abs[e])}
        return stuck

    def wait_all(self, eng, keys):
        deps = {k: self.cnt[k] for k in keys if self.cnt[k] > 0}
        self._emit_waits(eng, deps)


def build_program(n_ptiles=16, do_sample=True):
    nc = bass.Bass("TRN2", target_bir_lowering=False)
    S = Sched(nc)

    def din(name, shape):
        return nc.dram_tensor(name, list(shape), F32, kind="ExternalInput").ap()

    def dout(name, shape):
        return nc.dram_tensor(name, list(shape), F32, kind="ExternalOutput").ap()

    xp = din("xp", [2, SEQ, D])
    xsm = din("xs", [2, 16, D])
    ck = din("ck", [2, 128, 256])
    cv = din("cv", [2, 128, 256])
    w_in = din("w_in", [D, 5632])
    wa = din("wa", [D, D])
    wb = din("wb", [D, D])
    wo = din("wo", [D, D])
    w1 = din("w1", [D, 4096])
    w2 = din("w2", [4096, D])
    sgu_w = din("sgu_w", [8, 128, 128])
    sgu_b = din("sgu_b", [1, 1024])
    ln_g = din("ln_g", [1, 1024])
    ln_b = din("ln_b", [1, 1024])
    sinks = din("sinks", [1, 16])
    g1 = din("g1", [1, 1024])
    g2 = din("g2", [1, 1024])
    g3 = din("g3", [1, 1024])
    g4 = din("g4", [1, 1024])
    rope_d = din("rope", [128, 33, 32])
    ident_d = din("ident", [128, 128])

    yp = dout("yp", [2, SEQ, D])
    ys = dout("ys", [2, 16, D])
    kp = dout("kp", [2, 128, 256])
    vp = dout("vp", [2, 128, 256])
    ksn = dout("ksn", [2, 16, 256])
    vsn = dout("vsn", [2, 16, 256])
    vsg = dout("vsg", [2, 16, D])

    wscr = nc.dram_tensor("wscr", [NBLK, 128, 8, 512], BF16).ap()

    def sb(name, shape, dt):
        return nc.alloc_sbuf_tensor("sb_" + name, list(shape), dt)

    ident = sb("ident", [128, 128], BF16)
    WsT = sb("WsT", [128, 8, 128], BF16)
    BT = sb("BT", [128, 8, 128], F32)
    BTs = sb("BTs", [128, 8, 16], F32)
    lngT = sb("lngT", [128, 8], F32)
    g1T = sb("g1T", [128, 8], F32)
    g3T = sb("g3T", [128, 8], F32)
    g2B = sb("g2B", [128, 1024], F32)
    g4B = sb("g4B", [128, 1024], F32)
    rope = sb("rope", [128, 33, 32], F32)
    mhalf = sb("mhalf", [128, 8], F32)
    stat = sb("stat", [128, 64], F32)

    X = sb("X", [128, NXS, 1024], F32)
    nT = sb("nT", [128, 8, 512], BF16)
    tg = sb("tg", [128, 2, 8, 512], BF16)
    R32 = sb("R32", [128, 16384], BF16)
    qtok = sb("qtok", [128, 3, 1024], BF16)
    kf = sb("kf", [128, 256], F32)
    vf = sb("vf", [128, 256], F32)
    kb = sb("kb", [128, 4, 4, 128], BF16)
    Ke = sb("Ke", [128, 4, NKS * 128], BF16)
    Ko = sb("Ko", [128, 4, NKS * 128], BF16)
    Vx = sb("Vx", [128, NKS, 4, 128], BF16)
    PT = sb("PT", [128, 2, 2, 512], BF16)
    xsb = sb("xsb", [128, 3, 1024], BF16)
    junk = sb("junk", [128, 1024], BF16)
    btok = sb("btok", [128, 2, 1024], BF16)
    dstat = sb("dstat", [128, 64], F32)
    esB = sb("esB", [128, 16], F32)
    t1 = sb("t1", [128, 2, 512], F32)
    tmpF = sb("tmpF", [128, 1024], F32)
    relu_t = sb("relu_t", [128, 2, 512], F32)
    cA = t1[:, :, :].rearrange("p a c -> p (a c)")
    gv = cA
    cB = relu_t[:, :, :].rearrange("p a c -> p (a c)")
    rtmp = sb("rtmp", [128, 2, 8, 16], F32)
    ring = sb("ring", [128, NSLOT, 8, 512], BF16)
    ps = nc.alloc_psum_tensor("ps", [128, 8, 512], F32)
    print("SBUF bytes remaining per partition:", nc.sbuf_bytes_remaining, flush=True)

    hT = R32[:, :].rearrange("p (a t) -> p a t", t=512)
    guT = R32[:, 0:4096].rearrange("p (a t) -> p a t", t=512)
    vn = R32[:, 4096:8192].rearrange("p (a t) -> p a t", t=1024)
    QT = R32[:, 8192:12288].rearrange("p (a t) -> p a t", t=512)
    aT = R32[:, 12288:16384].rearrange("p (a t) -> p a t", t=512)
    bT = guT
    mT = QT

    r_const = Res("const")
    r_X = [Res("X%d" % i) for i in range(NXS)]
    r_nT = [Res("nT%d" % i) for i in range(4)]
    r_tg = [Res("tga"), Res("tgb")]
    r_gu = Res("guT/bT")
    r_vn = [Res("vn%d" % i) for i in range(4)]
    r_QT = Res("QT/mT")
    r_aT = Res("aT")
    r_hT = Res("hT")
    r_qtok = [[Res("qrope"), Res("qrest")] for _ in range(3)]
    r_kf = [Res("kfrope"), Res("kfrest")]
    r_vf = Res("vf")
    r_kb = [Res("kb%d" % i) for i in range(4)]
    r_K = [Res("K%d" % i) for i in range(NKS)]
    r_V = [Res("V%d" % i) for i in range(NKS)]
    r_PT = [[Res("PTA"), Res("PTB")] for _ in range(2)]
    r_xsb = [Res("xsb0"), Res("xsb1"), Res("xsb2")]
    r_junk = Res("junk")
    r_btok = [Res("btok0"), Res("btok1")]
    r_dstat = [Res("dstat0"), Res("dstat1")]
    r_t1 = [Res("t10"), Res("t11")]
    r_gv = r_t1
    r_rscr = [Res("rscr0"), Res("rscr1")]
    r_rsq = [Res("rsq%d" % i) for i in range(8)]
    rsq_ctr = [0]
    r_tmpF = Res("tmpF")
    r_relu = [Res("relu0"), Res("relu1")]
    r_rtmp = [Res("rtmp0"), Res("rtmp1")]
    r_ring = [Res("ring%d" % i) for i in range(NSLOT)]
    r_bank = [Res("bank%d" % i) for i in range(8)]
    r_stat = {}
    r_blk = [Res("scr%d" % i) for i in range(NBLK)]

    def rstat(name):
        if name not in r_stat:
            r_stat[name] = Res(name)
        return r_stat[name]

    out_sems = []

    bank_ptr = [0]

    def alloc1():
        b = bank_ptr[0]
        bank_ptr[0] = (b + 1) % 8
        return b

    def alloc2():
        if bank_ptr[0] % 2:
            bank_ptr[0] = (bank_ptr[0] + 1) % 8
        b = bank_ptr[0]
        bank_ptr[0] = (b + 2) % 8
        return b

    def tr_view(b):
        return ps[:, b, :].bitcast(BF16).rearrange("p (a t) -> p a t", t=128)

    def bc(t, off, nparts, pstep, dims):
        return bass.AP(t, off, [[pstep, nparts]] + [list(d) for d in dims])

    w_in_v = w_in.rearrange("(kc p) c -> p kc c", p=128)
    wa_v = wa.rearrange("(kc p) c -> p kc c", p=128)
    wb_v = wb.rearrange("(kc p) c -> p kc c", p=128)
    wo_v = wo.rearrange("(kc p) c -> p kc c", p=128)
    w1_v = w1.rearrange("(kc p) c -> p kc c", p=128)
    w2_v = w2.rearrange("(kc p) c -> p kc c", p=128)
    blk_src = []
    for c0 in (1024, 1536, 2048, 2560, 3072, 0, 512, 3584, 4096, 4608, 5120):
        blk_src.append(w_in_v[:, :, c0:c0 + 512])
    blk_src.append(wa_v[:, :, 0:512])
    blk_src.append(wb_v[:, :, 0:512])
    blk_src.append(wa_v[:, :, 512:1024])
    blk_src.append(wb_v[:, :, 512:1024])
    blk_src.append(wo_v[:, :, 0:512])
    blk_src.append(wo_v[:, :, 512:1024])
    for j in range(8):
        blk_src.append(w1_v[:, :, j * 512:(j + 1) * 512])
    for ch in range(2):
        for pc in range(4):
            blk_src.append(w2_v[:, pc * 8:(pc + 1) * 8, ch * 512:(ch + 1) * 512])
    assert len(blk_src) == NBLK

    prepass_done = [False] * NBLK

    def prepass(blist):
        for b in blist:
            if prepass_done[b]:
                continue
            prepass_done[b] = True
            S.dma("pool", "pp%d" % b,
                  lambda e, b=b: e.dma_start(out=wscr[b], in_=blk_src[b]),
                  writes=[r_blk[b]])

    ring_ctr = [0]

    def get_block(b):
        slot = ring_ctr[0] % NSLOT
        ring_ctr[0] += 1
        S.dma("sp", "ring%d" % slot,
              lambda e, b=b, slot=slot: e.dma_start(out=ring[:, slot, :, :], in_=wscr[b]),
              reads=[r_blk[b]], writes=[r_ring[slot]])
        return slot

    prepass(range(0, 11))

    def setup():
        S.dma("pool", "c_id", lambda e: e.dma_start(out=ident[:, :], in_=ident_d), writes=[r_const])
        S.dma("pool", "c_rope", lambda e: e.dma_start(out=rope[:, :, :], in_=rope_d), writes=[rstat("rope")])
        wnat = R32[:, 0:1024].rearrange("p (g j) -> p g j", j=128)
        S.dma("pool", "c_w", lambda e: e.dma_start(out=wnat, in_=sgu_w.rearrange("g i j -> i g j")),
              writes=[r_hT])
        b = alloc1()
        for g in range(8):
            S.op("pe", lambda e, g=g, b=b: e.transpose(tr_view(b)[:, g, :], wnat[:, g, :], ident[:, :]),
                 reads=[r_hT, r_const], writes=[r_bank[b]], inc=(g == 7))
        S.op("dve", lambda e, b=b: e.tensor_copy(out=WsT[:, :, :], in_=tr_view(b)),
             reads=[r_bank[b]], writes=[rstat("WsT")])
        S.op("dve", lambda e: e.memset(WsT[64:128, :, 0:64], 0.0), writes=[rstat("WsT")])
        def bload(key, dst, src, res):
            S.dma("pool", key, lambda e: e.dma_start(out=dst, in_=src.to_broadcast([128, 1024])),
                  writes=(res if isinstance(res, list) else [res]))
        bload("c_bs", cA[:, :], sgu_b, r_t1)
        bload("c_lb", cB[:, :], ln_b, r_relu)
        bload("c_g2", g2B[:, :], g2, rstat("g2B"))
        bload("c_g4", g4B[:, :], g4, rstat("g4B"))
        S.op("dve", lambda e: e.tensor_scalar(out=g2B[:, :], in0=g2B[:, :], scalar1=0.5, scalar2=None, op0=ALU.mult),
             reads=[rstat("g2B")], writes=[rstat("g2B")])
        LBb = R32[:, 1024:2048]
        r_lbb = Res("LBb")
        S.op("dve", lambda e: e.tensor_copy(out=LBb, in_=cB[:, :]), reads=r_relu, writes=[r_lbb])
        b0 = alloc2()
        for g in range(8):
            S.op("pe", lambda e, g=g, b0=b0: e.matmul(ps[:, b0 + g // 4, (g % 4) * 128:(g % 4) * 128 + 128],
                                                    lhsT=LBb[:, g * 128:(g + 1) * 128], rhs=WsT[:, g, :],
                                                    start=True, stop=True),
                 reads=[r_lbb, rstat("WsT")], writes=[r_bank[b0], r_bank[b0 + 1]], inc=(g == 7))
        psv = ps[:, b0:b0 + 2, :].rearrange("p a (g i) -> p (a g) i", i=128)
        S.op("dve", lambda e: e.tensor_tensor(out=BT[:, :, :], in0=psv,
                                              in1=cA[:, :].rearrange("p (g i) -> p g i", i=128), op=ALU.add),
             reads=[r_bank[b0], r_bank[b0 + 1]] + r_t1, writes=[rstat("BT")])
        b1 = alloc1()
        for g in range(8):
            S.op("pe", lambda e, g=g, b1=b1: e.matmul(ps[:, b1, g * 16:(g + 1) * 16],
                                                    lhsT=LBb[0:16, g * 128:(g + 1) * 128], rhs=WsT[0:16, g, 0:16],
                                                    start=True, stop=True),
                 reads=[r_lbb, rstat("WsT")], writes=[r_bank[b1]], inc=(g == 7))
        S.op("dve", lambda e: e.tensor_tensor(out=BTs[:, :, :],
                                              in0=ps[:, b1, 0:128].rearrange("p (g i) -> p g i", i=16),
                                              in1=cA[:, :].rearrange("p (g i) -> p g i", i=128)[:, :, 0:16], op=ALU.add),
             reads=[r_bank[b1]] + r_t1, writes=[rstat("BTs")])
        for key, dst, src in (("c_lgT", lngT, ln_g), ("c_g1T", g1T, g1), ("c_g3T", g3T, g3)):
            S.dma("pool", key, lambda e, dst=dst, src=src: e.dma_start(
                out=dst[:, :], in_=src.rearrange("o (g c) -> (o c) g", c=128), allow_slow_non_contiguous=True),
                writes=[rstat(key)])
        S.dma("pool", "c_sk", lambda e: e.dma_start(out=esB[:, :], in_=sinks.to_broadcast([128, 16])), writes=[rstat("esB")])
        S.op("act", lambda e: e.activation(out=esB[:, :], in_=esB[:, :], func=AF.Exp), reads=[rstat("esB")], writes=[rstat("esB")])
        S.op("dve", lambda e: e.memset(mhalf[:, :], -0.5), writes=[rstat("mhalf")])
        S.op("dve", lambda e: e.memset(Ke[64:128, :, :], 0.0), writes=r_K)
        S.op("dve", lambda e: e.memset(Ko[0:64, :, :], 0.0), writes=r_K)
        S.op("dve", lambda e: e.memset(Vx[:, :, :, 64:128], 1.0), writes=r_V)
        S.op("dve", lambda e: e.memset(PT[:, :, :, :], 0.0), writes=[r_PT[0][0], r_PT[0][1], r_PT[1][0], r_PT[1][1]])

    setup()

    xslot_ctr = [0]
    gtile = [0]

    class Tile:
        pass

    def rsqrt_chain(src_ap, dst_ap, nparts, ncols, scale, rsrc, rdst, key):
        ti = rsq_ctr[0] % 8
        rsq_ctr[0] += 1
        tkey = r_rsq[ti]
        tcol = stat[0:nparts, 41 + ti:41 + ti + ncols]
        S.op("dve", lambda e: e.tensor_scalar(out=tcol, in0=src_ap, scalar1=scale, scalar2=EPS, op0=ALU.mult, op1=ALU.add),
             reads=[rsrc], writes=[tkey])
        S.op("pool", lambda e: e.tensor_tensor(out=dst_ap, in0=tcol, in1=mhalf[0:nparts, 0:ncols], op=ALU.pow),
             reads=[tkey, rstat("mhalf")], writes=[rdst])

    def norm_stages(T, s, xsl, gT, gkey, ssq_col, rs_col):
        P = T.P
        xb = s % 3
        Xs = X[0:P, xsl, :]

        def st1():
            S.op("act", lambda e: e.activation(out=junk[0:P, :], in_=Xs, func=AF.Square, accum_out=stat[0:P, ssq_col:ssq_col + 1]),
                 reads=[r_X[xsl]], writes=[r_junk, rstat("ssq%d" % ssq_col)])
            rsqrt_chain(stat[0:P, ssq_col:ssq_col + 1], stat[0:P, rs_col:rs_col + 1], P, 1, 1.0 / D,
                        rstat("ssq%d" % ssq_col), rstat("rs%d" % rs_col), "rc%d" % rs_col)

        def st2a():
            S.op("act", lambda e: e.activation(out=xsb[0:P, xb, :], in_=Xs, func=AF.Copy, scale=stat[0:P, rs_col:rs_col + 1]),
                 reads=[r_X[xsl], rstat("rs%d" % rs_col)], writes=[r_xsb[xb]])

        def st2b():
            b = alloc1()
            for kc in range(8):
                S.op("pe", lambda e, kc=kc, b=b: e.transpose(tr_view(b)[:, kc, 0:P], xsb[0:P, xb, kc * 128:(kc + 1) * 128],
                                                            ident[0:P, 0:P]),
                     reads=[r_xsb[xb], r_const], writes=[r_bank[b]], inc=(kc == 7))
            S.op("dve", lambda e, b=b: e.tensor_tensor(out=nT[:, :, s * P:(s + 1) * P], in0=tr_view(b)[:, :, 0:P],
                                                      in1=bc(gT, 0, 128, 8, [[1, 8], [0, P]]), op=ALU.mult),
                 reads=[r_bank[b], rstat(gkey)], writes=[r_nT[s]])
        return st1, st2a, st2b

    def phaseA_stages(T, s):
        return norm_stages(T, s, T.xslots[s], g1T, "c_g1T", 16 + s, 20 + s)

    def rope_ops(T, s, src3, nh, dst3, r_src, r_dst_rope, r_dst_rest, rbuf):
        P = T.P
        blk = T.rope_blk(s)
        cc = bc(rope, blk * 32, P, 33 * 32, [[0, nh], [1, 16]])
        nsin = bc(rope, blk * 32 + 16, P, 33 * 32, [[0, nh], [1, 8]])
        psin = bc(rope, blk * 32 + 24, P, 33 * 32, [[0, nh], [1, 8]])
        A = rtmp[0:P, 0, 0:nh, :]
        B = rtmp[0:P, 1, 0:nh, :]
        S.op("dve", lambda e: e.tensor_tensor(out=A, in0=src3[:, :, 0:16], in1=cc, op=ALU.mult),
             reads=[r_src, rstat("rope")], writes=[r_rtmp[0]])
        S.op("dve", lambda e: e.tensor_tensor(out=B[:, :, 0:8], in0=src3[:, :, 8:16], in1=nsin, op=ALU.mult),
             reads=[r_src, rstat("rope")], writes=[r_rtmp[1]])
        S.op("dve", lambda e: e.tensor_tensor(out=B[:, :, 8:16], in0=src3[:, :, 0:8], in1=psin, op=ALU.mult),
             reads=[r_src, rstat("rope")], writes=[r_rtmp[1]])
        S.op("dve", lambda e: e.tensor_tensor(out=dst3[:, :, 0:16], in0=A, in1=B, op=ALU.add),
             reads=[r_rtmp[0], r_rtmp[1]], writes=[r_dst_rope])
        S.op("act", lambda e: e.activation(out=dst3[:, :, 16:64], in_=src3[:, :, 16:64], func=AF.Copy),
             reads=[r_src], writes=[r_dst_rest])

    def process_tile(T, Tnext):
        P, NS, NT = T.P, T.NS, T.NT
        first_tile = (gtile[0] == 0)
        gtile[0] += 1
        r_nT_all = r_nT[0:NS]

        if T.sample:
            lgs = T.lg_slot
            S.dma("pool", "c_lg", lambda e: e.dma_start(out=X[:, lgs, :], in_=ln_g.to_broadcast([128, 1024])), writes=[r_X[lgs]])
            S.dma("pool", "c_lb2", lambda e: e.dma_start(out=cB[:, :], in_=ln_b.to_broadcast([128, 1024])), writes=r_relu)

        todo = [s_ for s_ in range(NS) if s_ not in T.a_done]
        stg = [phaseA_stages(T, s_) for s_ in todo]
        n = len(todo)
        a_pending = {}
        for k in range(n + 1):
            if k < n:
                stg[k][0]()
            if k >= 1:
                stg[k - 1][1]()
                if T.a_done:
                    a_pending[todo[k - 1]] = stg[k - 1][2]
                else:
                    stg[k - 1][2]()

        pend_pe = []

        def tok_block(slots, epis):
            for s in range(NS):
                if s in a_pending:
                    a_pending.pop(s)()
                bl = []
                for slot in slots:
                    b = alloc1()
                    bl.append(b)
                    for kc in range(8):
                        S.op("pe", lambda e, kc=kc, b=b, s=s, slot=slot: e.matmul(ps[0:P, b, 0:512], lhsT=nT[:, kc, s * P:(s + 1) * P],
                                                                                rhs=ring[:, slot, kc, :], start=(kc == 0), stop=(kc == 7)),
                             reads=[r_nT[s], r_ring[slot]], writes=[r_bank[b]], inc=(kc == 7))
                if len(pend_pe) >= 1:
                    pend_pe.pop(0)()
                for epi, b in zip(epis, bl):
                    epi(s, b)

        def feat_block(slot, rhs_buf, r_rhs, epi, fc0):
            for fcl in range(4):
                b = alloc1()
                for kc in range(8):
                    S.op("pe", lambda e, kc=kc, b=b, fcl=fcl: e.matmul(ps[:, b, 0:NT], lhsT=ring[:, slot, kc, fcl * 128:(fcl + 1) * 128],
                                                                     rhs=rhs_buf[:, kc, 0:NT], start=(kc == 0), stop=(kc == 7)),
                         reads=r_rhs + [r_ring[slot]], writes=[r_bank[b]], inc=(kc == 7))
                epi(fc0 + fcl, b)

        def v_epi_half(half):
            def epi(s, b):
                S.op("act", lambda e: e.activation(out=gv[0:P, half * 512:(half + 1) * 512], in_=ps[0:P, b, 0:512], func=AF.Gelu),
                     reads=[r_bank[b]], writes=[r_gv[half]])
                if half == 1:
                    S.op("dve", lambda e: e.bn_stats(out=stat[0:P, 0:6], in_=gv[0:P, 0:512]), reads=[r_gv[0]], writes=[rstat("bn0")])
                    S.op("dve", lambda e: e.bn_stats(out=stat[0:P, 6:12], in_=gv[0:P, 512:1024]), reads=[r_gv[1]], writes=[rstat("bn1")])
                    S.op("dve", lambda e: e.bn_aggr(out=stat[0:P, 12:14], in_=stat[0:P, 0:12]),
                         reads=[rstat("bn0"), rstat("bn1")], writes=[rstat("mv")])
                    tkey = rstat("lnt")
                    S.op("dve", lambda e: e.tensor_scalar(out=stat[0:P, 14:15], in0=stat[0:P, 13:14], scalar1=EPS, scalar2=None, op0=ALU.add),
                         reads=[rstat("mv")], writes=[tkey])
                    S.op("pool", lambda e: e.tensor_tensor(out=stat[0:P, 15:16], in0=stat[0:P, 14:15], in1=mhalf[0:P, 0:1], op=ALU.pow),
                         reads=[tkey, rstat("mhalf")], writes=[rstat("lnrs")])
                    S.op("dve", lambda e: e.scalar_tensor_tensor(out=stat[0:P, 40:41], in0=stat[0:P, 12:13], scalar=-1.0, in1=stat[0:P, 15:16],
                                                                 op0=ALU.mult, op1=ALU.mult),
                         reads=[rstat("mv"), rstat("lnrs")], writes=[rstat("lnnb")])
                    if T.sample:
                        S.op("act", lambda e: e.activation(out=tmpF[0:P, :], in_=gv[0:P, :], func=AF.Identity,
                                                           bias=stat[0:P, 40:41], scale=stat[0:P, 15:16]),
                             reads=[r_gv[0], r_gv[1], rstat("lnrs"), rstat("lnnb")], writes=[r_tmpF])
                        S.op("dve", lambda e: e.tensor_copy(out=vn[0:P, s, :], in_=tmpF[0:P, :]), reads=[r_tmpF], writes=[r_vn[s]])
                        S.op("dve", lambda e: e.tensor_tensor(out=gv[0:P, :], in0=tmpF[0:P, :], in1=X[0:P, T.lg_slot, :], op=ALU.mult),
                             reads=[r_tmpF, r_X[T.lg_slot]], writes=[r_gv[0], r_gv[1]])
                        S.op("dve", lambda e: e.tensor_tensor(out=gv[0:P, :], in0=gv[0:P, :], in1=cB[0:P, :], op=ALU.add),
                             reads=[r_gv[0], r_gv[1]] + r_relu, writes=[r_gv[0], r_gv[1]])
                        S.dma("pool", "o_vsg", lambda e: e.dma_start(out=vsg[s, :, :], in_=gv[0:P, :]),
                              reads=[r_gv[0], r_gv[1]])
                    else:
                        S.op("act", lambda e: e.activation(out=vn[0:P, s, :], in_=gv[0:P, :], func=AF.Identity,
                                                           bias=stat[0:P, 40:41], scale=stat[0:P, 15:16]),
                             reads=[r_gv[0], r_gv[1], rstat("lnrs"), rstat("lnnb")], writes=[r_vn[s]])
            return epi

        if first_tile:
            pass
        sl0 = get_block(0)
        sl1 = get_block(1)
        tok_block([sl0, sl1], [v_epi_half(0), v_epi_half(1)])


        def q_epi_half(half):
            def epi(s, b):
                qb = s % 3
                src3 = ps[0:P, b, 0:512].rearrange("p (h d) -> p h d", d=64)
                dst3 = qtok[0:P, qb, half * 512:(half + 1) * 512].rearrange("p (h d) -> p h d", d=64)
                rope_ops(T, s, src3, 8, dst3, r_bank[b], r_qtok[qb][0], r_qtok[qb][1], 0)
                if half == 1:
                    def pe_part(s=s, qb=qb):
                        bb = alloc1()
                        for hp in range(8):
                            S.op("pe", lambda e, hp=hp, bb=bb: e.transpose(tr_view(bb)[:, hp, 0:P], qtok[0:P, qb, hp * 128:(hp + 1) * 128],
                                                                         ident[0:P, 0:P]),
                                 reads=[r_qtok[qb][0], r_qtok[qb][1], r_const], writes=[r_bank[bb]], inc=(hp == 7))
                        S.op("act", lambda e, bb=bb: e.activation(out=QT[:, :, s * P:(s + 1) * P], in_=tr_view(bb)[:, :, 0:P], func=AF.Copy),
                             reads=[r_bank[bb]], writes=[r_QT])
                    pend_pe.append(pe_part)
            return epi

        sl0 = get_block(2)
        sl1 = get_block(3)
        tok_block([sl0, sl1], [q_epi_half(0), q_epi_half(1)])

        def kv_epi(s, b):
            ks = T.kslot_new(s)
            nk = P
            src3 = ps[0:P, b, 0:256].rearrange("p (h d) -> p h d", d=64)
            dst3 = kf[0:P, :].rearrange("p (h d) -> p h d", d=64)
            rope_ops(T, s, src3, 4, dst3, r_bank[b], r_kf[0], r_kf[1], 1)
            kf3 = kf[0:P, :].rearrange("p (h d) -> p h d", d=64)
            S.op("dve", lambda e: e.tensor_copy(out=kb[0:P, s, :, 0:64], in_=kf3), reads=[r_kf[0], r_kf[1]], writes=[r_kb[s]])
            S.op("dve", lambda e: e.tensor_copy(out=kb[0:P, s, :, 64:128], in_=kf3), reads=[r_kf[0], r_kf[1]], writes=[r_kb[s]])
            S.op("act", lambda e: e.activation(out=Vx[0:P, ks, :, 0:64], in_=ps[0:P, b, 256:512].rearrange("p (h d) -> p h d", d=64),
                                               func=AF.Copy),
                 reads=[r_bank[b]], writes=[r_V[ks]])
            st = T.state_out(s)
            if st is not None:
                kdst, vdst, key = st
                S.op("act", lambda e: e.activation(out=vf[0:P, :], in_=ps[0:P, b, 256:512], func=AF.Copy),
                     reads=[r_bank[b]], writes=[r_vf])
                S.dma("pool", key + "k", lambda e: e.dma_start(out=kdst, in_=kf[0:P, :]), reads=[r_kf[0], r_kf[1]])
                S.dma("pool", key + "v", lambda e: e.dma_start(out=vdst, in_=vf[0:P, :]), reads=[r_vf])
                out_sems.extend([key + "k", key + "v"])

            def pe_part(s=s, ks=ks):
                bb = alloc1()
                for g in range(4):
                    S.op("pe", lambda e, g=g, bb=bb: e.transpose(tr_view(bb)[:, g, 0:P],
                                                               kb[0:P, s, g, :], ident[0:P, 0:P]),
                         reads=[r_kb[s], r_const], writes=[r_bank[bb]], inc=(g == 3))
                S.op("dve", lambda e, bb=bb: e.tensor_copy(out=Ke[0:64, :, ks * 128:ks * 128 + P], in_=tr_view(bb)[0:64, 0:4, 0:P]),
                     reads=[r_bank[bb]], writes=[r_K[ks]])
                S.op("act", lambda e, bb=bb: e.activation(out=Ko[64:128, :, ks * 128:ks * 128 + P], in_=tr_view(bb)[64:128, 0:4, 0:P], func=AF.Copy),
                     reads=[r_bank[bb]], writes=[r_K[ks]])
            pend_pe.append(pe_part)

        sl = get_block(4)
        tok_block([sl], [kv_epi])

        if T.sample:
            for s in range(NS):
                ka = T.kslot_cache(s)
                S.dma("pool", "ck_ld", lambda e, s=s: e.dma_start(out=tmpF[:, 0:256], in_=ck[s, :, :]), writes=[r_tmpF])
                r_kbc = Res("kbc")
                j3 = junk[:, 0:512].rearrange("p (h d) -> p h d", d=128)
                t3 = tmpF[:, 0:256].rearrange("p (h d) -> p h d", d=64)
                S.op("dve", lambda e, j3=j3, t3=t3: e.tensor_copy(out=j3[:, :, 0:64], in_=t3), reads=[r_tmpF], writes=[r_junk])
                S.op("dve", lambda e, j3=j3, t3=t3: e.tensor_copy(out=j3[:, :, 64:128], in_=t3), reads=[r_tmpF], writes=[r_junk])
                S.dma("pool", "cv_ld", lambda e, s=s: e.dma_start(out=tmpF[:, 256:512], in_=cv[s, :, :]), writes=[r_tmpF])
                S.op("act", lambda e, ka=ka: e.activation(out=Vx[:, ka, :, 0:64], in_=tmpF[:, 256:512].rearrange("p (h d) -> p h d", d=64),
                                                          func=AF.Copy),
                     reads=[r_tmpF], writes=[r_V[ka]])
                bb = alloc1()
                for g in range(4):
                    S.op("pe", lambda e, g=g, bb=bb: e.transpose(tr_view(bb)[:, g, :], junk[:, g * 128:(g + 1) * 128], ident[:, :]),
                         reads=[r_junk, r_const], writes=[r_bank[bb]], inc=(g == 3))
                S.op("dve", lambda e, bb=bb, ka=ka: e.tensor_copy(out=Ke[0:64, :, ka * 128:ka * 128 + 128], in_=tr_view(bb)[0:64, 0:4, :]),
                     reads=[r_bank[bb]], writes=[r_K[ka]])
                S.op("dve", lambda e, bb=bb, ka=ka: e.tensor_copy(out=Ko[64:128, :, ka * 128:ka * 128 + 128], in_=tr_view(bb)[64:128, 0:4, :]),
                     reads=[r_bank[bb]], writes=[r_K[ka]])

        def u_epi(fc, b):
            S.op("act", lambda e: e.activation(out=guT[:, fc, 0:NT], in_=ps[:, b, 0:NT], func=AF.Gelu),
                 reads=[r_bank[b]], writes=[r_gu])

        def gate_epi(which):
            def epi(fc, b):
                S.op("act", lambda e: e.activation(out=tg[:, which, fc, 0:NT], in_=ps[:, b, 0:NT], func=AF.Tanh, scale=0.5),
                     reads=[r_bank[b]], writes=[r_tg[which]])
            return epi

        fb = [(5, u_epi, 0), (6, u_epi, 4)]
        gate_blocks = [(7, gate_epi(0), 0), (8, gate_epi(0), 4), (9, gate_epi(1), 0), (10, gate_epi(1), 4)]
        for bi, epi, fc0 in fb:
            for _ in range(2):
                if pend_pe:
                    pend_pe.pop(0)()
            sl = get_block(bi)
            feat_block(sl, nT, r_nT_all, epi, fc0)
        while pend_pe:
            pend_pe.pop(0)()

        if first_tile:
            prepass(range(11, 17))

        for s in range(NS):
            b0 = alloc2()
            for g in range(8):
                S.op("pe", lambda e, g=g, b0=b0, s=s: e.matmul(ps[:, b0 + g // 4, (g % 4) * 128:(g % 4) * 128 + P],
                                                             lhsT=vn[0:P, s, g * 128:(g + 1) * 128], rhs=WsT[0:P, g, 0:P],
                                                             start=True, stop=True),
                     reads=[r_vn[s], rstat("WsT")], writes=[r_bank[b0], r_bank[b0 + 1]], inc=(g == 7))
            psv = ps[:, b0:b0 + 2, :].rearrange("p a (g i) -> p (a g) i", i=128)[:, :, 0:P]
            tS = tmpF[:, :].rearrange("p (g i) -> p g i", i=128)[:, :, 0:P]
            btv = BTs[:, :, 0:P] if T.sample else BT[:, :, 0:P]
            btk = "BTs" if T.sample else "BT"
            S.op("dve", lambda e, psv=psv, tS=tS: e.tensor_tensor(out=tS, in0=psv, in1=bc(lngT, 0, 128, 8, [[1, 8], [0, P]]), op=ALU.mult),
                 reads=[r_bank[b0], r_bank[b0 + 1], rstat("c_lgT")], writes=[r_tmpF])
            S.op("dve", lambda e, tS=tS, btv=btv: e.tensor_tensor(out=tS, in0=tS, in1=btv, op=ALU.add),
                 reads=[r_tmpF, rstat(btk)], writes=[r_tmpF])
            S.op("dve", lambda e, tS=tS, s=s: e.tensor_tensor(out=aT[:, :, s * P:(s + 1) * P], in0=tS, in1=guT[:, :, s * P:(s + 1) * P], op=ALU.mult),
                 reads=[r_tmpF, r_gu], writes=[r_aT])

        NC4 = 4 * P
        bt_pending = []
        gate_ctr = [0]
        gate_slot = [None]
        for s in range(NS):
            kts = T.keytiles(s)
            obase = 0
            sbase = 4
            ob = s % 2
            pv_pending = []
            for g in range(4):
                pb = g % 2
                for (xi, slot, nk, regions) in kts:
                    b = sbase + xi
                    for par, Kt in enumerate((Ke, Ko)):
                        S.op("pe", lambda e, b=b, par=par, Kt=Kt, slot=slot, nk=nk, g=g, s=s: e.matmul(
                            ps[0:nk, b, par * 2 * P:(par + 1) * 2 * P].rearrange("p (a q) -> p a q", q=P),
                            lhsT=Kt[:, g, slot * 128:slot * 128 + nk],
                            rhs=QT[:, 2 * g:2 * g + 2, s * P:(s + 1) * P], start=True, stop=True),
                            reads=[r_K[slot], r_QT], writes=[r_bank[b]], inc=(par == 1))
                    for (r0, r1, q0, q1) in regions:
                        S.op("act", lambda e, b=b, xi=xi, r0=r0, r1=r1, q0=q0, q1=q1, pb=pb: e.activation(
                            out=PT[r0:r1, pb, xi, 0:NC4].rearrange("p (a q) -> p a q", q=P)[:, :, q0:q1],
                            in_=ps[r0:r1, b, 0:NC4].rearrange("p (a q) -> p a q", q=P)[:, :, q0:q1],
                            func=AF.Exp, scale=0.125),
                            reads=[r_bank[b]], writes=[r_PT[pb][xi]])
                for _ in range(4 // NS):
                    gci = gate_ctr[0]
                    gate_ctr[0] += 1
                    bi, epi, fc0 = gate_blocks[gci // 4]
                    fcl = gci % 4
                    if fcl == 0:
                        gate_slot[0] = get_block(bi)
                    gsl = gate_slot[0]
                    gb_ = sbase + 2 + (gci % 2)
                    for kc in range(8):
                        S.op("pe", lambda e, kc=kc, gb_=gb_, fcl=fcl, gsl=gsl: e.matmul(ps[:, gb_, 0:NT], lhsT=ring[:, gsl, kc, fcl * 128:(fcl + 1) * 128],
                                                                                   rhs=nT[:, kc, 0:NT], start=(kc == 0), stop=(kc == 7)),
                             reads=r_nT_all + [r_ring[gsl]], writes=[r_bank[gb_]], inc=(kc == 7))
                    epi(fc0 + fcl, gb_)
                if g == 1 and bt_pending:
                    bt_pending.pop(0)()

                def pv(g=g, pb=pb, kts=kts, s=s, obase=obase):
                    bo = obase + g
                    for sl in range(4):
                        par, pl = sl // 2, sl % 2
                        j = 2 * pl + par
                        for ki, (xi, slot, nk, regions) in enumerate(kts):
                            last = (ki == len(kts) - 1)
                            S.op("pe", lambda e, bo=bo, xi=xi, slot=slot, nk=nk, ki=ki, last=last, sl=sl, j=j: e.matmul(
                                ps[0:P, bo, j * 65:j * 65 + 65], lhsT=PT[0:nk, pb, xi, sl * P:(sl + 1) * P], rhs=Vx[0:nk, slot, g, 0:65],
                                start=(ki == 0), stop=last),
                                reads=[r_V[slot], r_PT[pb][xi]], writes=[r_bank[bo]], inc=(last and sl == 3))
                if pv_pending:
                    pv_pending.pop(0)()
                pv_pending.append(pv)
            while pv_pending:
                pv_pending.pop(0)()
            obanks = [r_bank[obase + i] for i in range(4)]
            dv = bass.AP(ps, obase * 512 + 64, [[4096, P], [512, 4], [65, 4]])
            ov = bass.AP(ps, obase * 512, [[4096, P], [512, 4], [65, 4], [1, 64]])
            ds_ = dstat[0:P, ob * 32:ob * 32 + 16]
            rr_ = dstat[0:P, ob * 32 + 16:ob * 32 + 32]
            S.op("dve", lambda e, dv=dv, ds_=ds_: e.tensor_tensor(out=ds_.rearrange("p (a b) -> p a b", b=4), in0=dv,
                                                                 in1=esB[0:P, :].rearrange("p (a b) -> p a b", b=4), op=ALU.add),
                 reads=obanks + [rstat("esB")], writes=[r_dstat[ob]])
            S.op("dve", lambda e, ds_=ds_, rr_=rr_: e.reciprocal(out=rr_, in_=ds_), reads=[r_dstat[ob]], writes=[r_dstat[ob]])
            S.op("dve", lambda e, ov=ov, rr_=rr_, ob=ob: e.tensor_tensor(
                out=btok[0:P, ob, :].rearrange("p (a b d) -> p a b d", b=4, d=64), in0=ov,
                in1=bc(dstat, ob * 32 + 16, P, 64, [[4, 4], [1, 4], [0, 64]]), op=ALU.mult),
                reads=obanks + [r_dstat[ob]], writes=[r_btok[ob]])

            def bt_part(s=s, ob=ob, obase=obase):
                tb_ = obase + 3
                for hp in range(8):
                    S.op("pe", lambda e, hp=hp: e.transpose(tr_view(tb_)[:, hp, 0:P], btok[0:P, ob, hp * 128:(hp + 1) * 128], ident[0:P, 0:P]),
                         reads=[r_btok[ob], r_const], writes=[r_bank[tb_]], inc=(hp == 7))
                S.op("dve", lambda e: e.tensor_copy(out=bT[:, :, s * P:(s + 1) * P], in_=tr_view(tb_)[:, :, 0:P]),
                     reads=[r_bank[tb_]], writes=[r_gu])
            bt_pending.append(bt_part)
        while bt_pending:
            bt_pending.pop(0)()
        bank_ptr[0] = 0

        if first_tile:
            prepass(range(17, 25))

        for half in range(2):
            sla = get_block(11 + 2 * half)
            slb = get_block(12 + 2 * half)
            bas, bbs = [], []
            for fcl in range(4):
                ba = alloc1()
                bas.append(ba)
                for kc in range(8):
                    S.op("pe", lambda e, kc=kc, ba=ba, fcl=fcl, sla=sla: e.matmul(ps[:, ba, 0:NT], lhsT=ring[:, sla, kc, fcl * 128:(fcl + 1) * 128],
                                                                               rhs=aT[:, kc, 0:NT], start=(kc == 0), stop=(kc == 7)),
                         reads=[r_aT, r_ring[sla]], writes=[r_bank[ba]], inc=(kc == 7))
            for fcl in range(4):
                bbk = alloc1()
                bbs.append(bbk)
                for kc in range(8):
                    S.op("pe", lambda e, kc=kc, bbk=bbk, fcl=fcl, slb=slb: e.matmul(ps[:, bbk, 0:NT], lhsT=ring[:, slb, kc, fcl * 128:(fcl + 1) * 128],
                                                                                 rhs=bT[:, kc, 0:NT], start=(kc == 0), stop=(kc == 7)),
                         reads=[r_gu, r_ring[slb]], writes=[r_bank[bbk]], inc=(kc == 7))
                fc = half * 4 + fcl
                ba = bas[fcl]
                tb = fc % 2
                S.op("dve", lambda e, fc=fc, ba=ba, tb=tb: e.scalar_tensor_tensor(out=t1[:, tb, 0:NT], in0=tg[:, 0, fc, 0:NT], scalar=1.0,
                                                                                 in1=ps[:, ba, 0:NT], op0=ALU.add, op1=ALU.mult),
                     reads=[r_tg[0], r_bank[ba]], writes=[r_t1[tb]])
                S.op("dve", lambda e, fc=fc, bbk=bbk, tb=tb: e.scalar_tensor_tensor(out=relu_t[:, tb, 0:NT], in0=tg[:, 1, fc, 0:NT], scalar=1.0,
                                                                                   in1=ps[:, bbk, 0:NT], op0=ALU.add, op1=ALU.mult),
                     reads=[r_tg[1], r_bank[bbk]], writes=[r_relu[tb]])
                S.op("dve", lambda e, fc=fc, tb=tb: e.tensor_tensor(out=mT[:, fc, 0:NT], in0=t1[:, tb, 0:NT], in1=relu_t[:, tb, 0:NT], op=ALU.add),
                     reads=[r_t1[tb], r_relu[tb]], writes=[r_QT])

        if first_tile:
            prepass(range(25, 33))

        slo = [get_block(15), get_block(16)]
        fb0 = {}

        def f_st1(s):
            b0 = alloc2()
            fb0[s] = b0
            for cb in range(2):
                for kc in range(8):
                    S.op("pe", lambda e, kc=kc, cb=cb, b0=b0, s=s: e.matmul(ps[0:P, b0 + cb, 0:512], lhsT=mT[:, kc, s * P:(s + 1) * P],
                                                                          rhs=ring[:, slo[cb], kc, :], start=(kc == 0), stop=(kc == 7)),
                         reads=[r_QT, r_ring[slo[cb]]], writes=[r_bank[b0 + cb]], inc=(kc == 7))

        def f_st1_epi(s):
            b0 = fb0[s]
            pz = ps[0:P, b0:b0 + 2, :].rearrange("p a c -> p (a c)")
            c_ssq, c_rs = 24 + s, 28 + s
            S.op("act", lambda e, pz=pz, c_ssq=c_ssq: e.activation(out=junk[0:P, :], in_=pz, func=AF.Square, scale=0.5,
                                                                  accum_out=stat[0:P, c_ssq:c_ssq + 1]),
                 reads=[r_bank[b0], r_bank[b0 + 1]], writes=[r_junk, rstat("ssq%d" % c_ssq)])
            rsqrt_chain(stat[0:P, c_ssq:c_ssq + 1], stat[0:P, c_rs:c_rs + 1], P, 1, 1.0 / D,
                        rstat("ssq%d" % c_ssq), rstat("rs%d" % c_rs), "rc%d" % c_rs)

        def f_st2(s):
            xsl = T.xslots[s]
            b0 = fb0[s]
            pz = ps[0:P, b0:b0 + 2, :].rearrange("p a c -> p (a c)")
            c_rs = 28 + s
            tbuf, tres = (tmpF, [r_tmpF]) if s % 2 == 0 else (gv, [r_gv[0], r_gv[1]])
            S.op("dve", lambda e, pz=pz, c_rs=c_rs, tbuf=tbuf: e.scalar_tensor_tensor(out=tbuf[0:P, :], in0=pz, scalar=stat[0:P, c_rs:c_rs + 1], in1=g2B[0:P, :],
                                                                                     op0=ALU.mult, op1=ALU.mult),
                 reads=[r_bank[b0], r_bank[b0 + 1], rstat("rs%d" % c_rs), rstat("g2B")], writes=tres)
            S.op("dve", lambda e, xsl=xsl, tbuf=tbuf: e.tensor_tensor(out=X[0:P, xsl, :], in0=X[0:P, xsl, :], in1=tbuf[0:P, :], op=ALU.add),
                 reads=tres + [r_X[xsl]], writes=[r_X[xsl]])

        fstg = [norm_stages(T, s_, T.xslots[s_], g3T, "c_g3T", 32 + s_, 36 + s_) for s_ in range(NS)]
        for k in range(NS + 3):
            if k < NS:
                f_st1(k)
            if 0 <= k - 1 < NS:
                f_st2(k - 1)
            if k < NS:
                f_st1_epi(k)
            if 0 <= k - 3 < NS:
                fstg[k - 3][2]()
            if 0 <= k - 1 < NS:
                fstg[k - 1][0]()
            if 0 <= k - 2 < NS:
                fstg[k - 2][1]()

        def h_epi(fc, b):
            tb = fc % 2
            S.op("act", lambda e: e.activation(out=relu_t[:, tb, 0:NT], in_=ps[:, b, 0:NT], func=AF.Relu),
                 reads=[r_bank[b]], writes=[r_relu[tb]])
            S.op("dve", lambda e: e.tensor_tensor(out=hT[:, fc, 0:NT], in0=relu_t[:, tb, 0:NT], in1=relu_t[:, tb, 0:NT], op=ALU.mult),
                 reads=[r_relu[tb]], writes=[r_hT, r_gu, r_QT, r_aT] + r_vn)

        hoist = []
        for j in range(8):
            if j == 5 and Tnext is not None:
                hs = [s_ for s_ in range(Tnext.NS)]
                hoist = [phaseA_stages(Tnext, s_) for s_ in hs]
                Tnext.a_done = set(hs)
                for st in hoist:
                    st[0]()
                for st in hoist[0:3]:
                    st[1]()
            sl = get_block(17 + j)
            feat_block(sl, nT, r_nT_all, h_epi, j * 4)

        hbanks = [[None] * NS for _ in range(2)]
        for st in hoist[0:3]:
            st[2]()
        for st in hoist[3:]:
            st[1]()
            st[2]()
        for ch in range(2):
            for s in range(NS):
                hbanks[ch][s] = alloc1()
            for pc in range(4):
                sl = get_block(25 + ch * 4 + pc)
                for s in range(NS):
                    b = hbanks[ch][s]
                    for kcl in range(8):
                        kc = pc * 8 + kcl
                        S.op("pe", lambda e, kc=kc, kcl=kcl, b=b, s=s, sl=sl: e.matmul(ps[0:P, b, 0:512], lhsT=hT[:, kc, s * P:(s + 1) * P],
                                                                                     rhs=ring[:, sl, kcl, :], start=(kc == 0), stop=(kc == 31)),
                             reads=[r_hT, r_gu, r_QT, r_aT] + r_vn + [r_ring[sl]], writes=[r_bank[b]], inc=(kcl == 7))
            for s in range(NS):
                b = hbanks[ch][s]
                col = 44 + ch
                S.op("act", lambda e, b=b, s=s, ch=ch: e.activation(out=junk[0:P, 0:512], in_=ps[0:P, b, 0:512], func=AF.Square,
                                                                  accum_out=stat[0:P, 52 + ch * 4 + s:53 + ch * 4 + s]),
                     reads=[r_bank[b]], writes=[r_junk, rstat("hss%d_%d" % (ch, s))])
        for s in range(NS):
            S.op("dve", lambda e, s=s: e.tensor_tensor(out=stat[0:P, 52 + s:53 + s], in0=stat[0:P, 52 + s:53 + s], in1=stat[0:P, 56 + s:57 + s], op=ALU.add),
                 reads=[rstat("hss0_%d" % s), rstat("hss1_%d" % s)], writes=[rstat("hss0_%d" % s)])
            rsqrt_chain(stat[0:P, 52 + s:53 + s], stat[0:P, 56 + s:57 + s], P, 1, 1.0 / D,
                        rstat("hss0_%d" % s), rstat("hss1_%d" % s), "hrc%d" % s)
        for s in range(NS):
            xsl = T.xslots[s]
            tbuf, tres = (tmpF, [r_tmpF]) if s % 2 == 0 else (gv, [r_gv[0], r_gv[1]])
            for ch in range(2):
                b = hbanks[ch][s]
                S.op("dve", lambda e, b=b, s=s, ch=ch, tbuf=tbuf: e.scalar_tensor_tensor(out=tbuf[0:P, ch * 512:(ch + 1) * 512], in0=ps[0:P, b, 0:512],
                                                                                        scalar=stat[0:P, 56 + s:57 + s], in1=g4B[0:P, ch * 512:(ch + 1) * 512],
                                                                                        op0=ALU.mult, op1=ALU.mult),
                     reads=[r_bank[b], rstat("hss1_%d" % s), rstat("g4B")], writes=tres)
            S.op("dve", lambda e, xsl=xsl, tbuf=tbuf: e.tensor_tensor(out=X[0:P, xsl, :], in0=X[0:P, xsl, :], in1=tbuf[0:P, :], op=ALU.add),
                 reads=tres + [r_X[xsl]], writes=[r_X[xsl]])
            dst = T.y_dst(s)
            S.dma("pool", "xst%d" % xsl, lambda e, dst=dst, xsl=xsl: e.dma_start(out=dst, in_=X[0:P, xsl, :]), reads=[r_X[xsl]])
            if ("xst%d" % xsl) not in out_sems:
                out_sems.append("xst%d" % xsl)

    def load_x(T, s_list):
        for s in s_list:
            xsl = T.xslots[s]
            src = T.x_src(s)
            S.dma("pool", "xld%d" % xsl, lambda e, src=src, xsl=xsl, P=T.P: e.dma_start(out=X[0:P, xsl, :], in_=src), writes=[r_X[xsl]])

    tiles = []
    for ti in range(n_ptiles):
        T = Tile()
        T.a_done = set()
        T.sample = False
        T.P, T.NS, T.NT = 128, 4, 512
        seq, tt = ti // 8, ti % 8
        T.seq, T.tt = seq, tt
        T.xslots = [(xslot_ctr[0] + s) % NXS for s in range(4)]
        xslot_ctr[0] += 4
        T.rope_blk = (lambda s, tt=tt: tt * 4 + s)
        T.x_src = (lambda s, seq=seq, tt=tt: xp[seq, tt * 512 + s * 128: tt * 512 + (s + 1) * 128, :])
        T.y_dst = (lambda s, seq=seq, tt=tt: yp[seq, tt * 512 + s * 128: tt * 512 + (s + 1) * 128, :])
        T.kslot_new = (lambda s, tt=tt: (tt * 4 + s) % NKS)

        def keytiles(s, tt=tt):
            gs = tt * 4 + s
            res = []
            if gs > 0:
                res.append((0, (gs - 1) % NKS, 128, [(0, 128, 0, 64), (64, 128, 64, 128)]))
            res.append((1, gs % NKS, 128, [(0, 64, 0, 64), (0, 128, 64, 128)]))
            return res
        T.keytiles = keytiles

        def state_out(s, seq=seq, tt=tt):
            if tt == 7 and s == 3:
                return (kp[seq, :, :], vp[seq, :, :], "o_p%d" % seq)
            return None
        T.state_out = state_out
        tiles.append(T)

    if do_sample:
        T = Tile()
        T.a_done = set()
        T.sample = True
        T.P, T.NS, T.NT = 16, 2, 32
        T.xslots = [(xslot_ctr[0] + s) % NXS for s in range(2)]
        T.lg_slot = (xslot_ctr[0] + 2) % NXS
        xslot_ctr[0] += 2
        T.rope_blk = (lambda s: 32)
        T.x_src = (lambda s: xsm[s, :, :])
        T.y_dst = (lambda s: ys[s, :, :])
        T.kslot_new = (lambda s: 2 * s + 1)
        T.kslot_cache = (lambda s: 2 * s)
        T.keytiles = (lambda s: [(0, 2 * s, 128, [(0, 128, 0, 16)]), (1, 2 * s + 1, 16, [(0, 16, 0, 16)])])
        T.state_out = (lambda s: (ksn[s, :, :], vsn[s, :, :], "o_s%d" % s))
        tiles.append(T)

    for i, T in enumerate(tiles):
        if i == 0:
            load_x(T, range(T.NS))
        if i + 1 < len(tiles):
            load_x(tiles[i + 1], range(tiles[i + 1].NS))
        process_tile(T, tiles[i + 1] if i + 1 < len(tiles) else None)

    if do_sample:
        out_sems.append("o_vsg")
    S.wait_all("pool", [k for k in dict.fromkeys(out_sems)])

    stuck = S.check_deadlock()
    if stuck:
        raise RuntimeError("static deadlock detected: %r" % (stuck,))
    with nc.Block() as block:
        @block.tensor
        def _(e):
            for f in S.prog["pe"]:
                f(e)

        @block.scalar
        def _(e):
            for f in S.prog["act"]:
                f(e)

        @block.vector
        def _(e):
            for f in S.prog["dve"]:
                f(e)

        @block.gpsimd
        def _(e):
            for f in S.prog["pool"]:
                f(e)

        @block.sync
        def _(e):
            for f in S.prog["sp"]:
                f(e)
    return nc


def _rope_table():
    half = 8
    inv = (np.float32(500000.0) ** (-np.arange(half, dtype=np.float32) * np.float32(2.0) / np.float32(16))).astype(np.float32)
    tab = np.zeros((128, 33, 32), np.float32)
    p = np.arange(128)
    for blk in range(33):
        if blk < 32:
            pos = (blk * 128 + p).astype(np.float32)
        else:
            pos = (4096 + (p % 16)).astype(np.float32)
        ang = (pos[:, None] * inv[None, :]).astype(np.float32)
        c = np.cos(ang).astype(np.float32)
        s = np.sin(ang).astype(np.float32)
        tab[:, blk, 0:8] = c
        tab[:, blk, 8:16] = c
        tab[:, blk, 16:24] = -s
        tab[:, blk, 24:32] = s
    return tab


_NC_CACHE = {}


def kernel(x_prompt, x_sample, cache_swa_k, cache_swa_v, w_in, sgu_ln_g, sgu_ln_b, sgu_w, sgu_b, attn_sinks,
           w_branch_a, w_branch_b, w_out, g_mix_pre, g_mix_post, g_ffn_pre, g_ffn_post, w_ff1, w_ff2,
           _n_ptiles=16, _do_sample=True):
    f = lambda a: np.ascontiguousarray(np.asarray(a, dtype=np.float32))
    x_prompt = np.asarray(x_prompt, dtype=np.float32)
    x_sample = f(x_sample)
    ckf = f(cache_swa_k)[0].reshape(16, 128, 256)
    cvf = f(cache_swa_v)[0].reshape(16, 128, 256)
    shared = {
        "w_in": f(w_in)[0], "wa": f(w_branch_a)[0], "wb": f(w_branch_b)[0], "wo": f(w_out)[0],
        "w1": f(w_ff1)[0], "w2": f(w_ff2)[0], "sgu_w": f(sgu_w)[0], "sgu_b": f(sgu_b)[0].reshape(1, 1024),
        "ln_g": f(sgu_ln_g)[0].reshape(1, 1024), "ln_b": f(sgu_ln_b)[0].reshape(1, 1024),
        "sinks": f(attn_sinks)[0].reshape(1, 16),
        "g1": f(g_mix_pre)[0].reshape(1, 1024), "g2": f(g_mix_post)[0].reshape(1, 1024),
        "g3": f(g_ffn_pre)[0].reshape(1, 1024), "g4": f(g_ffn_post)[0].reshape(1, 1024),
        "rope": _rope_table(), "ident": np.eye(128, dtype=np.float32),
    }
    in_maps = []
    for c in range(NCORES):
        m = dict(shared)
        m["xp"] = np.ascontiguousarray(x_prompt[2 * c:2 * c + 2])
        m["xs"] = np.ascontiguousarray(x_sample[2 * c:2 * c + 2])
        m["ck"] = np.ascontiguousarray(ckf[2 * c:2 * c + 2])
        m["cv"] = np.ascontiguousarray(cvf[2 * c:2 * c + 2])
        in_maps.append(m)
    key = (_n_ptiles, _do_sample)
    if key not in _NC_CACHE:
        _NC_CACHE[key] = build_program(_n_ptiles, _do_sample)
    nc = _NC_CACHE[key]
    res = run_bass_kernel_spmd(nc, in_maps, core_ids=list(range(NCORES)))
    R = res.results
    y_prompt = np.concatenate([r["yp"] for r in R], axis=0)
    y_sample = np.concatenate([r["ys"] for r in R], axis=0)
    kpo = np.concatenate([r["kp"] for r in R], axis=0).reshape(1, 16, 128, 4, 64)
    vpo = np.concatenate([r["vp"] for r in R], axis=0).reshape(1, 16, 128, 4, 64)
    kso = np.concatenate([r["ksn"] for r in R], axis=0).reshape(1, 16, 16, 4, 64)
    vso = np.concatenate([r["vsn"] for r in R], axis=0).reshape(1, 16, 16, 4, 64)
    vsgo = np.concatenate([r["vsg"] for r in R], axis=0).reshape(1, 16, 16, 1024)
    return (y_prompt, y_sample, kpo, vpo, kso, vso, vsgo)
```

```python
import numpy as np
import concourse.bass as bass
import concourse.mybir as mybir
from concourse.bass_utils import run_bass_kernel_spmd

F32 = mybir.dt.float32
BF16 = mybir.dt.bfloat16
AF = mybir.ActivationFunctionType
ALU = mybir.AluOpType

NCORES = 8
SEQ = 4096
D = 1024
EPS = 1e-6
NSLOT = 5
NXS = 8
NKS = 6
NBLK = 33
SAME_ENGINE_WAW = False


class Res:
    __slots__ = ("w", "r", "name")

    def __init__(self, name=""):
        self.w = {}
        self.r = {}
        self.name = name


class Sched:
    ENG = ("pe", "act", "dve", "pool", "sp")

    def __init__(self, nc):
        self.nc = nc
        self.prog = {e: [] for e in self.ENG}
        self.abs = {e: [] for e in self.ENG}
        self.sem = {}
        self.cnt = {}
        self.waited = {e: {} for e in self.ENG}
        for e in ("pe", "act", "dve", "pool"):
            self.newsem(e)

    def newsem(self, key):
        self.sem[key] = self.nc.alloc_semaphore("s_" + key)
        self.cnt[key] = 0

    def _deps(self, eng, reads, writes):
        deps = {}
        for r in reads:
            for k, v in r.w.items():
                if k == eng and eng == "pe":
                    continue
                if deps.get(k, 0) < v:
                    deps[k] = v
        for w in writes:
            for k, v in w.w.items():
                if k == eng and (eng == "pe" or not SAME_ENGINE_WAW):
                    continue
                if deps.get(k, 0) < v:
                    deps[k] = v
            for k, v in w.r.items():
                if k == eng and (eng == "pe" or not SAME_ENGINE_WAW):
                    continue
                if deps.get(k, 0) < v:
                    deps[k] = v
        return deps

    def _emit_waits(self, eng, deps):
        wd = self.waited[eng]
        for k, v in deps.items():
            if wd.get(k, 0) < v:
                wd[k] = v
                h = self.sem[k]
                self.prog[eng].append(lambda e, h=h, v=v: e.wait_ge(h, v))
                self.abs[eng].append(("wait", k, v))

    def op(self, eng, fn, reads=(), writes=(), inc=True):
        deps = self._deps(eng, reads, writes)
        self._emit_waits(eng, deps)
        if inc:
            self.cnt[eng] += 1
            v = self.cnt[eng]
            h = self.sem[eng]
            self.prog[eng].append(lambda e, fn=fn, h=h: fn(e).then_inc(h, 1))
            self.abs[eng].append(("inc", eng, 1))
        else:
            v = self.cnt[eng] + 1
            self.prog[eng].append(lambda e, fn=fn: fn(e))
        for w in writes:
            w.w = {eng: v}
            w.r = {}
        for r in reads:
            if r.r.get(eng, 0) < v:
                r.r[eng] = v

    def dma(self, qeng, semkey, fn, reads=(), writes=()):
        if semkey not in self.sem:
            self.newsem(semkey)
        deps = self._deps(None, reads, writes)
        self._emit_waits(qeng, deps)
        self.cnt[semkey] += 16
        v = self.cnt[semkey]
        h = self.sem[semkey]
        self.prog[qeng].append(lambda e, fn=fn, h=h: fn(e).then_inc(h, 16))
        self.abs[qeng].append(("inc", semkey, 16))
        for w in writes:
            w.w = {semkey: v}
            w.r = {}
        for r in reads:
            if r.r.get(semkey, 0) < v:
                r.r[semkey] = v

    def check_deadlock(self):
        val = {k: 0 for k in self.sem}
        pc = {e: 0 for e in self.ENG}
        progress = True
        while progress:
            progress = False
            for e in self.ENG:
                lst = self.abs[e]
                while pc[e] < len(lst):
                    kind, k, v = lst[pc[e]]
                    if kind == "wait":
                        if val[k] >= v:
                            pc[e] += 1
                            progress = True
                        else:
                            break
                    else:
                        val[k] += v
                        pc[e] += 1
                        progress = True
        stuck = {e: (pc[e], len(self.abs[e]), self.abs[e][pc[e]] if pc[e] < len(self.abs[e]) else None, val.get(self.abs[e][pc[e]][1]) if pc[e] < len(self.abs[e]) else None)
                 for e in self.ENG if pc[e] < len(self.abs[e])}
        return stuck

    def wait_all(self, eng, keys):
        deps = {k: self.cnt[k] for k in keys if self.cnt[k] > 0}
        self._emit_waits(eng, deps)


def build_program(n_ptiles=16, do_sample=True):
    nc = bass.Bass("TRN2", target_bir_lowering=False)
    S = Sched(nc)

    def din(name, shape):
        return nc.dram_tensor(name, list(shape), F32, kind="ExternalInput").ap()

    def dout(name, shape):
        return nc.dram_tensor(name, list(shape), F32, kind="ExternalOutput").ap()

    xp = din("xp", [2, SEQ, D])
    xsm = din("xs", [2, 16, D])
    ck = din("ck", [2, 128, 256])
    cv = din("cv", [2, 128, 256])
    w_in = din("w_in", [D, 5632])
    wa = din("wa", [D, D])
    wb = din("wb", [D, D])
    wo = din("wo", [D, D])
    w1 = din("w1", [D, 4096])
    w2 = din("w2", [4096, D])
    sgu_w = din("sgu_w", [8, 128, 128])
    sgu_b = din("sgu_b", [1, 1024])
    ln_g = din("ln_g", [1, 1024])
    ln_b = din("ln_b", [1, 1024])
    sinks = din("sinks", [1, 16])
    g1 = din("g1", [1, 1024])
    g2 = din("g2", [1, 1024])
    g3 = din("g3", [1, 1024])
    g4 = din("g4", [1, 1024])
    rope_d = din("rope", [128, 33, 32])
    ident_d = din("ident", [128, 128])

    yp = dout("yp", [2, SEQ, D])
    ys = dout("ys", [2, 16, D])
    kp = dout("kp", [2, 128, 256])
    vp = dout("vp", [2, 128, 256])
    ksn = dout("ksn", [2, 16, 256])
    vsn = dout("vsn", [2, 16, 256])
    vsg = dout("vsg", [2, 16, D])

    wscr = nc.dram_tensor("wscr", [NBLK, 128, 8, 512], BF16).ap()

    def sb(name, shape, dt):
        return nc.alloc_sbuf_tensor("sb_" + name, list(shape), dt)

    ident = sb("ident", [128, 128], BF16)
    WsT = sb("WsT", [128, 8, 128], BF16)
    BT = sb("BT", [128, 8, 128], F32)
    BTs = sb("BTs", [128, 8, 16], F32)
    lngT = sb("lngT", [128, 8], F32)
    g1T = sb("g1T", [128, 8], F32)
    g3T = sb("g3T", [128, 8], F32)
    g2B = sb("g2B", [128, 1024], F32)
    g4B = sb("g4B", [128, 1024], F32)
    rope = sb("rope", [128, 33, 32], F32)
    mhalf = sb("mhalf", [128, 8], F32)
    stat = sb("stat", [128, 64], F32)

    X = sb("X", [128, NXS, 1024], F32)
    nT = sb("nT", [128, 8, 512], BF16)
    tg = sb("tg", [128, 2, 8, 512], BF16)
    R32 = sb("R32", [128, 16384], BF16)
    qtok = sb("qtok", [128, 3, 1024], BF16)
    kf = sb("kf", [128, 256], F32)
    vf = sb("vf", [128, 256], F32)
    kb = sb("kb", [128, 4, 4, 128], BF16)
    Ke = sb("Ke", [128, 4, NKS * 128], BF16)
    Ko = sb("Ko", [128, 4, NKS * 128], BF16)
    Vx = sb("Vx", [128, NKS, 4, 128], BF16)
    PT = sb("PT", [128, 2, 2, 512], BF16)
    xsb = sb("xsb", [128, 3, 1024], BF16)
    junk = sb("junk", [128, 1024], BF16)
    btok = sb("btok", [128, 2, 1024], BF16)
    dstat = sb("dstat", [128, 64], F32)
    esB = sb("esB", [128, 16], F32)
    t1 = sb("t1", [128, 2, 512], F32)
    tmpF = sb("tmpF", [128, 1024], F32)
    relu_t = sb("relu_t", [128, 2, 512], F32)
    cA = t1[:, :, :].rearrange("p a c -> p (a c)")
    gv = cA
    cB = relu_t[:, :, :].rearrange("p a c -> p (a c)")
    rtmp = sb("rtmp", [128, 2, 8, 16], F32)
    ring = sb("ring", [128, NSLOT, 8, 512], BF16)
    ps = nc.alloc_psum_tensor("ps", [128, 8, 512], F32)
    print("SBUF bytes remaining per partition:", nc.sbuf_bytes_remaining, flush=True)

    hT = R32[:, :].rearrange("p (a t) -> p a t", t=512)
    guT = R32[:, 0:4096].rearrange("p (a t) -> p a t", t=512)
    vn = R32[:, 4096:8192].rearrange("p (a t) -> p a t", t=1024)
    QT = R32[:, 8192:12288].rearrange("p (a t) -> p a t", t=512)
    aT = R32[:, 12288:16384].rearrange("p (a t) -> p a t", t=512)
    bT = guT
    mT = QT

    r_const = Res("const")
    r_X = [Res("X%d" % i) for i in range(NXS)]
    r_nT = [Res("nT%d" % i) for i in range(4)]
    r_tg = [Res("tga"), Res("tgb")]
    r_gu = Res("guT/bT")
    r_vn = [Res("vn%d" % i) for i in range(4)]
    r_QT = Res("QT/mT")
    r_aT = Res("aT")
    r_hT = Res("hT")
    r_qtok = [[Res("qrope"), Res("qrest")] for _ in range(3)]
    r_kf = [Res("kfrope"), Res("kfrest")]
    r_vf = Res("vf")
    r_kb = [Res("kb%d" % i) for i in range(4)]
    r_K = [Res("K%d" % i) for i in range(NKS)]
    r_V = [Res("V%d" % i) for i in range(NKS)]
    r_PT = [[Res("PTA"), Res("PTB")] for _ in range(2)]
    r_xsb = [Res("xsb0"), Res("xsb1"), Res("xsb2")]
    r_junk = Res("junk")
    r_btok = [Res("btok0"), Res("btok1")]
    r_dstat = [Res("dstat0"), Res("dstat1")]
    r_t1 = [Res("t10"), Res("t11")]
    r_gv = r_t1
    r_rscr = [Res("rscr0"), Res("rscr1")]
    r_rsq = [Res("rsq%d" % i) for i in range(8)]
    rsq_ctr = [0]
    r_tmpF = Res("tmpF")
    r_relu = [Res("relu0"), Res("relu1")]
    r_rtmp = [Res("rtmp0"), Res("rtmp1")]
    r_ring = [Res("ring%d" % i) for i in range(NSLOT)]
    r_bank = [Res("bank%d" % i) for i in range(8)]
    r_stat = {}
    r_blk = [Res("scr%d" % i) for i in range(NBLK)]

    def rstat(name):
        if name not in r_stat:
            r_stat[name] = Res(name)
        return r_stat[name]

    out_sems = []

    bank_ptr = [0]

    def alloc1():
        b = bank_ptr[0]
        bank_ptr[0] = (b + 1) % 8
        return b

    def alloc2():
        if bank_ptr[0] % 2:
            bank_ptr[0] = (bank_ptr[0] + 1) % 8
        b = bank_ptr[0]
        bank_ptr[0] = (b + 2) % 8
        return b

    def tr_view(b):
        return ps[:, b, :].bitcast(BF16).rearrange("p (a t) -> p a t", t=128)

    def bc(t, off, nparts, pstep, dims):
        return bass.AP(t, off, [[pstep, nparts]] + [list(d) for d in dims])

    w_in_v = w_in.rearrange("(kc p) c -> p kc c", p=128)
    wa_v = wa.rearrange("(kc p) c -> p kc c", p=128)
    wb_v = wb.rearrange("(kc p) c -> p kc c", p=128)
    wo_v = wo.rearrange("(kc p) c -> p kc c", p=128)
    w1_v = w1.rearrange("(kc p) c -> p kc c", p=128)
    w2_v = w2.rearrange("(kc p) c -> p kc c", p=128)
    blk_src = []
    for c0 in (1024, 1536, 2048, 2560, 3072, 0, 512, 3584, 4096, 4608, 5120):
        blk_src.append(w_in_v[:, :, c0:c0 + 512])
    blk_src.append(wa_v[:, :, 0:512])
    blk_src.append(wb_v[:, :, 0:512])
    blk_src.append(wa_v[:, :, 512:1024])
    blk_src.append(wb_v[:, :, 512:1024])
    blk_src.append(wo_v[:, :, 0:512])
    blk_src.append(wo_v[:, :, 512:1024])
    for j in range(8):
        blk_src.append(w1_v[:, :, j * 512:(j + 1) * 512])
    for ch in range(2):
        for pc in range(4):
            blk_src.append(w2_v[:, pc * 8:(pc + 1) * 8, ch * 512:(ch + 1) * 512])
    assert len(blk_src) == NBLK

    prepass_done = [False] * NBLK

    def prepass(blist):
        for b in blist:
            if prepass_done[b]:
                continue
            prepass_done[b] = True
            S.dma("pool", "pp%d" % b,
                  lambda e, b=b: e.dma_start(out=wscr[b], in_=blk_src[b]),
                  writes=[r_blk[b]])

    ring_ctr = [0]

    def get_block(b):
        slot = ring_ctr[0] % NSLOT
        ring_ctr[0] += 1
        S.dma("sp", "ring%d" % slot,
              lambda e, b=b, slot=slot: e.dma_start(out=ring[:, slot, :, :], in_=wscr[b]),
              reads=[r_blk[b]], writes=[r_ring[slot]])
        return slot

    prepass(range(0, 11))

    def setup():
        S.dma("pool", "c_id", lambda e: e.dma_start(out=ident[:, :], in_=ident_d), writes=[r_const])
        S.dma("pool", "c_rope", lambda e: e.dma_start(out=rope[:, :, :], in_=rope_d), writes=[rstat("rope")])
        wnat = R32[:, 0:1024].rearrange("p (g j) -> p g j", j=128)
        S.dma("pool", "c_w", lambda e: e.dma_start(out=wnat, in_=sgu_w.rearrange("g i j -> i g j")),
              writes=[r_hT])
        b = alloc1()
        for g in range(8):
            S.op("pe", lambda e, g=g, b=b: e.transpose(tr_view(b)[:, g, :], wnat[:, g, :], ident[:, :]),
                 reads=[r_hT, r_const], writes=[r_bank[b]], inc=(g == 7))
        S.op("dve", lambda e, b=b: e.tensor_copy(out=WsT[:, :, :], in_=tr_view(b)),
             reads=[r_bank[b]], writes=[rstat("WsT")])
        S.op("dve", lambda e: e.memset(WsT[64:128, :, 0:64], 0.0), writes=[rstat("WsT")])
        def bload(key, dst, src, res):
            S.dma("pool", key, lambda e: e.dma_start(out=dst, in_=src.to_broadcast([128, 1024])),
                  writes=(res if isinstance(res, list) else [res]))
        bload("c_bs", cA[:, :], sgu_b, r_t1)
        bload("c_lb", cB[:, :], ln_b, r_relu)
        bload("c_g2", g2B[:, :], g2, rstat("g2B"))
        bload("c_g4", g4B[:, :], g4, rstat("g4B"))
        S.op("dve", lambda e: e.tensor_scalar(out=g2B[:, :], in0=g2B[:, :], scalar1=0.5, scalar2=None, op0=ALU.mult),
             reads=[rstat("g2B")], writes=[rstat("g2B")])
        LBb = R32[:, 1024:2048]
        r_lbb = Res("LBb")
        S.op("dve", lambda e: e.tensor_copy(out=LBb, in_=cB[:, :]), reads=r_relu, writes=[r_lbb])
        b0 = alloc2()
        for g in range(8):
            S.op("pe", lambda e, g=g, b0=b0: e.matmul(ps[:, b0 + g // 4, (g % 4) * 128:(g % 4) * 128 + 128],
                                                    lhsT=LBb[:, g * 128:(g + 1) * 128], rhs=WsT[:, g, :],
                                                    start=True, stop=True),
                 reads=[r_lbb, rstat("WsT")], writes=[r_bank[b0], r_bank[b0 + 1]], inc=(g == 7))
        psv = ps[:, b0:b0 + 2, :].rearrange("p a (g i) -> p (a g) i", i=128)
        S.op("dve", lambda e: e.tensor_tensor(out=BT[:, :, :], in0=psv,
                                              in1=cA[:, :].rearrange("p (g i) -> p g i", i=128), op=ALU.add),
             reads=[r_bank[b0], r_bank[b0 + 1]] + r_t1, writes=[rstat("BT")])
        b1 = alloc1()
        for g in range(8):
            S.op("pe", lambda e, g=g, b1=b1: e.matmul(ps[:, b1, g * 16:(g + 1) * 16],
                                                    lhsT=LBb[0:16, g * 128:(g + 1) * 128], rhs=WsT[0:16, g, 0:16],
                                                    start=True, stop=True),
                 reads=[r_lbb, rstat("WsT")], writes=[r_bank[b1]], inc=(g == 7))
        S.op("dve", lambda e: e.tensor_tensor(out=BTs[:, :, :],
                                              in0=ps[:, b1, 0:128].rearrange("p (g i) -> p g i", i=16),
                                              in1=cA[:, :].rearrange("p (g i) -> p g i", i=128)[:, :, 0:16], op=ALU.add),
             reads=[r_bank[b1]] + r_t1, writes=[rstat("BTs")])
        for key, dst, src in (("c_lgT", lngT, ln_g), ("c_g1T", g1T, g1), ("c_g3T", g3T, g3)):
            S.dma("pool", key, lambda e, dst=dst, src=src: e.dma_start(
                out=dst[:, :], in_=src.rearrange("o (g c) -> (o c) g", c=128), allow_slow_non_contiguous=True),
                writes=[rstat(key)])
        S.dma("pool", "c_sk", lambda e: e.dma_start(out=esB[:, :], in_=sinks.to_broadcast([128, 16])), writes=[rstat("esB")])
        S.op("act", lambda e: e.activation(out=esB[:, :], in_=esB[:, :], func=AF.Exp), reads=[rstat("esB")], writes=[rstat("esB")])
        S.op("dve", lambda e: e.memset(mhalf[:, :], -0.5), writes=[rstat("mhalf")])
        S.op("dve", lambda e: e.memset(Ke[64:128, :, :], 0.0), writes=r_K)
        S.op("dve", lambda e: e.memset(Ko[0:64, :, :], 0.0), writes=r_K)
        S.op("dve", lambda e: e.memset(Vx[:, :, :, 64:128], 1.0), writes=r_V)
        S.op("dve", lambda e: e.memset(PT[:, :, :, :], 0.0), writes=[r_PT[0][0], r_PT[0][1], r_PT[1][0], r_PT[1][1]])

    setup()

    xslot_ctr = [0]
    gtile = [0]

    class Tile:
        pass

    def rsqrt_chain(src_ap, dst_ap, nparts, ncols, scale, rsrc, rdst, key):
        ti = rsq_ctr[0] % 8
        rsq_ctr[0] += 1
        tkey = r_rsq[ti]
        tcol = stat[0:nparts, 41 + ti:41 + ti + ncols]
        S.op("dve", lambda e: e.tensor_scalar(out=tcol, in0=src_ap, scalar1=scale, scalar2=EPS, op0=ALU.mult, op1=ALU.add),
             reads=[rsrc], writes=[tkey])
        S.op("pool", lambda e: e.tensor_tensor(out=dst_ap, in0=tcol, in1=mhalf[0:nparts, 0:ncols], op=ALU.pow),
             reads=[tkey, rstat("mhalf")], writes=[rdst])

    def norm_stages(T, s, xsl, gT, gkey, ssq_col, rs_col):
        P = T.P
        xb = s % 3
        Xs = X[0:P, xsl, :]

        def st1():
            S.op("act", lambda e: e.activation(out=junk[0:P, :], in_=Xs, func=AF.Square, accum_out=stat[0:P, ssq_col:ssq_col + 1]),
                 reads=[r_X[xsl]], writes=[r_junk, rstat("ssq%d" % ssq_col)])
            rsqrt_chain(stat[0:P, ssq_col:ssq_col + 1], stat[0:P, rs_col:rs_col + 1], P, 1, 1.0 / D,
                        rstat("ssq%d" % ssq_col), rstat("rs%d" % rs_col), "rc%d" % rs_col)

        def st2a():
            S.op("act", lambda e: e.activation(out=xsb[0:P, xb, :], in_=Xs, func=AF.Copy, scale=stat[0:P, rs_col:rs_col + 1]),
                 reads=[r_X[xsl], rstat("rs%d" % rs_col)], writes=[r_xsb[xb]])

        def st2b():
            b = alloc1()
            for kc in range(8):
                S.op("pe", lambda e, kc=kc, b=b: e.transpose(tr_view(b)[:, kc, 0:P], xsb[0:P, xb, kc * 128:(kc + 1) * 128],
                                                            ident[0:P, 0:P]),
                     reads=[r_xsb[xb], r_const], writes=[r_bank[b]], inc=(kc == 7))
            S.op("dve", lambda e, b=b: e.tensor_tensor(out=nT[:, :, s * P:(s + 1) * P], in0=tr_view(b)[:, :, 0:P],
                                                      in1=bc(gT, 0, 128, 8, [[1, 8], [0, P]]), op=ALU.mult),
                 reads=[r_bank[b], rstat(gkey)], writes=[r_nT[s]])
        return st1, st2a, st2b

    def phaseA_stages(T, s):
        return norm_stages(T, s, T.xslots[s], g1T, "c_g1T", 16 + s, 20 + s)

    def rope_ops(T, s, src3, nh, dst3, r_src, r_dst_rope, r_dst_rest, rbuf):
        P = T.P
        blk = T.rope_blk(s)
        cc = bc(rope, blk * 32, P, 33 * 32, [[0, nh], [1, 16]])
        nsin = bc(rope, blk * 32 + 16, P, 33 * 32, [[0, nh], [1, 8]])
        psin = bc(rope, blk * 32 + 24, P, 33 * 32, [[0, nh], [1, 8]])
        A = rtmp[0:P, 0, 0:nh, :]
        B = rtmp[0:P, 1, 0:nh, :]
        S.op("dve", lambda e: e.tensor_tensor(out=A, in0=src3[:, :, 0:16], in1=cc, op=ALU.mult),
             reads=[r_src, rstat("rope")], writes=[r_rtmp[0]])
        S.op("dve", lambda e: e.tensor_tensor(out=B[:, :, 0:8], in0=src3[:, :, 8:16], in1=nsin, op=ALU.mult),
             reads=[r_src, rstat("rope")], writes=[r_rtmp[1]])
        S.op("dve", lambda e: e.tensor_tensor(out=B[:, :, 8:16], in0=src3[:, :, 0:8], in1=psin, op=ALU.mult),
             reads=[r_src, rstat("rope")], writes=[r_rtmp[1]])
        S.op("dve", lambda e: e.tensor_tensor(out=dst3[:, :, 0:16], in0=A, in1=B, op=ALU.add),
             reads=[r_rtmp[0], r_rtmp[1]], writes=[r_dst_rope])
        S.op("act", lambda e: e.activation(out=dst3[:, :, 16:64], in_=src3[:, :, 16:64], func=AF.Copy),
             reads=[r_src], writes=[r_dst_rest])

    def process_tile(T, Tnext):
        P, NS, NT = T.P, T.NS, T.NT
        first_tile = (gtile[0] == 0)
        gtile[0] += 1
        r_nT_all = r_nT[0:NS]

        if T.sample:
            lgs = T.lg_slot
            S.dma("pool", "c_lg", lambda e: e.dma_start(out=X[:, lgs, :], in_=ln_g.to_broadcast([128, 1024])), writes=[r_X[lgs]])
            S.dma("pool", "c_lb2", lambda e: e.dma_start(out=cB[:, :], in_=ln_b.to_broadcast([128, 1024])), writes=r_relu)

        todo = [s_ for s_ in range(NS) if s_ not in T.a_done]
        stg = [phaseA_stages(T, s_) for s_ in todo]
        n = len(todo)
        a_pending = {}
        for k in range(n + 1):
            if k < n:
                stg[k][0]()
            if k >= 1:
                stg[k - 1][1]()
                if T.a_done:
                    a_pending[todo[k - 1]] = stg[k - 1][2]
                else:
                    stg[k - 1][2]()

        pend_pe = []

        def tok_block(slots, epis):
            for s in range(NS):
                if s in a_pending:
                    a_pending.pop(s)()
                bl = []
                for slot in slots:
                    b = alloc1()
                    bl.append(b)
                    for kc in range(8):
                        S.op("pe", lambda e, kc=kc, b=b, s=s, slot=slot: e.matmul(ps[0:P, b, 0:512], lhsT=nT[:, kc, s * P:(s + 1) * P],
                                                                                rhs=ring[:, slot, kc, :], start=(kc == 0), stop=(kc == 7)),
                             reads=[r_nT[s], r_ring[slot]], writes=[r_bank[b]], inc=(kc == 7))
                if len(pend_pe) >= 1:
                    pend_pe.pop(0)()
                for epi, b in zip(epis, bl):
                    epi(s, b)

        def feat_block(slot, rhs_buf, r_rhs, epi, fc0):
            for fcl in range(4):
                b = alloc1()
                for kc in range(8):
                    S.op("pe", lambda e, kc=kc, b=b, fcl=fcl: e.matmul(ps[:, b, 0:NT], lhsT=ring[:, slot, kc, fcl * 128:(fcl + 1) * 128],
                                                                     rhs=rhs_buf[:, kc, 0:NT], start=(kc == 0), stop=(kc == 7)),
                         reads=r_rhs + [r_ring[slot]], writes=[r_bank[b]], inc=(kc == 7))
                epi(fc0 + fcl, b)

        def v_epi_half(half):
            def epi(s, b):
                S.op("act", lambda e: e.activation(out=gv[0:P, half * 512:(half + 1) * 512], in_=ps[0:P, b, 0:512], func=AF.Gelu),
                     reads=[r_bank[b]], writes=[r_gv[half]])
                if half == 1:
                    S.op("dve", lambda e: e.bn_stats(out=stat[0:P, 0:6], in_=gv[0:P, 0:512]), reads=[r_gv[0]], writes=[rstat("bn0")])
                    S.op("dve", lambda e: e.bn_stats(out=stat[0:P, 6:12], in_=gv[0:P, 512:1024]), reads=[r_gv[1]], writes=[rstat("bn1")])
                    S.op("dve", lambda e: e.bn_aggr(out=stat[0:P, 12:14], in_=stat[0:P, 0:12]),
                         reads=[rstat("bn0"), rstat("bn1")], writes=[rstat("mv")])
                    tkey = rstat("lnt")
                    S.op("dve", lambda e: e.tensor_scalar(out=stat[0:P, 14:15], in0=stat[0:P, 13:14], scalar1=EPS, scalar2=None, op0=ALU.add),
                         reads=[rstat("mv")], writes=[tkey])
                    S.op("pool", lambda e: e.tensor_tensor(out=stat[0:P, 15:16], in0=stat[0:P, 14:15], in1=mhalf[0:P, 0:1], op=ALU.pow),
                         reads=[tkey, rstat("mhalf")], writes=[rstat("lnrs")])
                    S.op("dve", lambda e: e.scalar_tensor_tensor(out=stat[0:P, 40:41], in0=stat[0:P, 12:13], scalar=-1.0, in1=stat[0:P, 15:16],
                                                                 op0=ALU.mult, op1=ALU.mult),
                         reads=[rstat("mv"), rstat("lnrs")], writes=[rstat("lnnb")])
                    if T.sample:
                        S.op("act", lambda e: e.activation(out=tmpF[0:P, :], in_=gv[0:P, :], func=AF.Identity,
                                                           bias=stat[0:P, 40:41], scale=stat[0:P, 15:16]),
                             reads=[r_gv[0], r_gv[1], rstat("lnrs"), rstat("lnnb")], writes=[r_tmpF])
                        S.op("dve", lambda e: e.tensor_copy(out=vn[0:P, s, :], in_=tmpF[0:P, :]), reads=[r_tmpF], writes=[r_vn[s]])
                        S.op("dve", lambda e: e.tensor_tensor(out=gv[0:P, :], in0=tmpF[0:P, :], in1=X[0:P, T.lg_slot, :], op=ALU.mult),
                             reads=[r_tmpF, r_X[T.lg_slot]], writes=[r_gv[0], r_gv[1]])
                        S.op("dve", lambda e: e.tensor_tensor(out=gv[0:P, :], in0=gv[0:P, :], in1=cB[0:P, :], op=ALU.add),
                             reads=[r_gv[0], r_gv[1]] + r_relu, writes=[r_gv[0], r_gv[1]])
                        S.dma("pool", "o_vsg", lambda e: e.dma_start(out=vsg[s, :, :], in_=gv[0:P, :]),
                              reads=[r_gv[0], r_gv[1]])
                    else:
                        S.op("act", lambda e: e.activation(out=vn[0:P, s, :], in_=gv[0:P, :], func=AF.Identity,
                                                           bias=stat[0:P, 40:41], scale=stat[0:P, 15:16]),
                             reads=[r_gv[0], r_gv[1], rstat("lnrs"), rstat("lnnb")], writes=[r_vn[s]])
            return epi

        if first_tile:
            pass
        sl0 = get_block(0)
        sl1 = get_block(1)
        tok_block([sl0, sl1], [v_epi_half(0), v_epi_half(1)])


        def q_epi_half(half):
            def epi(s, b):
                qb = s % 3
                src3 = ps[0:P, b, 0:512].rearrange("p (h d) -> p h d", d=64)
                dst3 = qtok[0:P, qb, half * 512:(half + 1) * 512].rearrange("p (h d) -> p h d", d=64)
                rope_ops(T, s, src3, 8, dst3, r_bank[b], r_qtok[qb][0], r_qtok[qb][1], 0)
                if half == 1:
                    def pe_part(s=s, qb=qb):
                        bb = alloc1()
                        for hp in range(8):
                            S.op("pe", lambda e, hp=hp, bb=bb: e.transpose(tr_view(bb)[:, hp, 0:P], qtok[0:P, qb, hp * 128:(hp + 1) * 128],
                                                                         ident[0:P, 0:P]),
                                 reads=[r_qtok[qb][0], r_qtok[qb][1], r_const], writes=[r_bank[bb]], inc=(hp == 7))
                        S.op("act", lambda e, bb=bb: e.activation(out=QT[:, :, s * P:(s + 1) * P], in_=tr_view(bb)[:, :, 0:P], func=AF.Copy),
                             reads=[r_bank[bb]], writes=[r_QT])
                    pend_pe.append(pe_part)
            return epi

        sl0 = get_block(2)
        sl1 = get_block(3)
        tok_block([sl0, sl1], [q_epi_half(0), q_epi_half(1)])

        def kv_epi(s, b):
            ks = T.kslot_new(s)
            nk = P
            src3 = ps[0:P, b, 0:256].rearrange("p (h d) -> p h d", d=64)
            dst3 = kf[0:P, :].rearrange("p (h d) -> p h d", d=64)
            rope_ops(T, s, src3, 4, dst3, r_bank[b], r_kf[0], r_kf[1], 1)
            kf3 = kf[0:P, :].rearrange("p (h d) -> p h d", d=64)
            S.op("dve", lambda e: e.tensor_copy(out=kb[0:P, s, :, 0:64], in_=kf3), reads=[r_kf[0], r_kf[1]], writes=[r_kb[s]])
            S.op("dve", lambda e: e.tensor_copy(out=kb[0:P, s, :, 64:128], in_=kf3), reads=[r_kf[0], r_kf[1]], writes=[r_kb[s]])
            S.op("act", lambda e: e.activation(out=Vx[0:P, ks, :, 0:64], in_=ps[0:P, b, 256:512].rearrange("p (h d) -> p h d", d=64),
                                               func=AF.Copy),
                 reads=[r_bank[b]], writes=[r_V[ks]])
            st = T.state_out(s)
            if st is not None:
                kdst, vdst, key = st
                S.op("act", lambda e: e.activation(out=vf[0:P, :], in_=ps[0:P, b, 256:512], func=AF.Copy),
                     reads=[r_bank[b]], writes=[r_vf])
                S.dma("pool", key + "k", lambda e: e.dma_start(out=kdst, in_=kf[0:P, :]), reads=[r_kf[0], r_kf[1]])
                S.dma("pool", key + "v", lambda e: e.dma_start(out=vdst, in_=vf[0:P, :]), reads=[r_vf])
                out_sems.extend([key + "k", key + "v"])

            def pe_part(s=s, ks=ks):
                bb = alloc1()
                for g in range(4):
                    S.op("pe", lambda e, g=g, bb=bb: e.transpose(tr_view(bb)[:, g, 0:P],
                                                               kb[0:P, s, g, :], ident[0:P, 0:P]),
                         reads=[r_kb[s], r_const], writes=[r_bank[bb]], inc=(g == 3))
                S.op("dve", lambda e, bb=bb: e.tensor_copy(out=Ke[0:64, :, ks * 128:ks * 128 + P], in_=tr_view(bb)[0:64, 0:4, 0:P]),
                     reads=[r_bank[bb]], writes=[r_K[ks]])
                S.op("act", lambda e, bb=bb: e.activation(out=Ko[64:128, :, ks * 128:ks * 128 + P], in_=tr_view(bb)[64:128, 0:4, 0:P], func=AF.Copy),
                     reads=[r_bank[bb]], writes=[r_K[ks]])
            pend_pe.append(pe_part)

        sl = get_block(4)
        tok_block([sl], [kv_epi])

        if T.sample:
            for s in range(NS):
                ka = T.kslot_cache(s)
                S.dma("pool", "ck_ld", lambda e, s=s: e.dma_start(out=tmpF[:, 0:256], in_=ck[s, :, :]), writes=[r_tmpF])
                r_kbc = Res("kbc")
                j3 = junk[:, 0:512].rearrange("p (h d) -> p h d", d=128)
                t3 = tmpF[:, 0:256].rearrange("p (h d) -> p h d", d=64)
                S.op("dve", lambda e, j3=j3, t3=t3: e.tensor_copy(out=j3[:, :, 0:64], in_=t3), reads=[r_tmpF], writes=[r_junk])
                S.op("dve", lambda e, j3=j3, t3=t3: e.tensor_copy(out=j3[:, :, 64:128], in_=t3), reads=[r_tmpF], writes=[r_junk])
                S.dma("pool", "cv_ld", lambda e, s=s: e.dma_start(out=tmpF[:, 256:512], in_=cv[s, :, :]), writes=[r_tmpF])
                S.op("act", lambda e, ka=ka: e.activation(out=Vx[:, ka, :, 0:64], in_=tmpF[:, 256:512].rearrange("p (h d) -> p h d", d=64),
                                                          func=AF.Copy),
                     reads=[r_tmpF], writes=[r_V[ka]])
                bb = alloc1()
                for g in range(4):
                    S.op("pe", lambda e, g=g, bb=bb: e.transpose(tr_view(bb)[:, g, :], junk[:, g * 128:(g + 1) * 128], ident[:, :]),
                         reads=[r_junk, r_const], writes=[r_bank[bb]], inc=(g == 3))
                S.op("dve", lambda e, bb=bb, ka=ka: e.tensor_copy(out=Ke[0:64, :, ka * 128:ka * 128 + 128], in_=tr_view(bb)[0:64, 0:4, :]),
                     reads=[r_bank[bb]], writes=[r_K[ka]])
                S.op("dve", lambda e, bb=bb, ka=ka: e.tensor_copy(out=Ko[64:128, :, ka * 128:ka * 128 + 128], in_=tr_view(bb)[64:128, 0:4, :]),
                     reads=[r_bank[bb]], writes=[r_K[ka]])

        def u_epi(fc, b):
            S.op("act", lambda e: e.activation(out=guT[:, fc, 0:NT], in_=ps[:, b, 0:NT], func=AF.Gelu),
                 reads=[r_bank[b]], writes=[r_gu])

        def gate_epi(which):
            def epi(fc, b):
                S.op("act", lambda e: e.activation(out=tg[:, which, fc, 0:NT], in_=ps[:, b, 0:NT], func=AF.Tanh, scale=0.5),
                     reads=[r_bank[b]], writes=[r_tg[which]])
            return epi

        fb = [(5, u_epi, 0), (6, u_epi, 4)]
        gate_blocks = [(7, gate_epi(0), 0), (8, gate_epi(0), 4), (9, gate_epi(1), 0), (10, gate_epi(1), 4)]
        for bi, epi, fc0 in fb:
            for _ in range(2):
                if pend_pe:
                    pend_pe.pop(0)()
            sl = get_block(bi)
            feat_block(sl, nT, r_nT_all, epi, fc0)
        while pend_pe:
            pend_pe.pop(0)()

        if first_tile:
            prepass(range(11, 17))

        for s in range(NS):
            b0 = alloc2()
            for g in range(8):
                S.op("pe", lambda e, g=g, b0=b0, s=s: e.matmul(ps[:, b0 + g // 4, (g % 4) * 128:(g % 4) * 128 + P],
                                                             lhsT=vn[0:P, s, g * 128:(g + 1) * 128], rhs=WsT[0:P, g, 0:P],
                                                             start=True, stop=True),
                     reads=[r_vn[s], rstat("WsT")], writes=[r_bank[b0], r_bank[b0 + 1]], inc=(g == 7))
            psv = ps[:, b0:b0 + 2, :].rearrange("p a (g i) -> p (a g) i", i=128)[:, :, 0:P]
            tS = tmpF[:, :].rearrange("p (g i) -> p g i", i=128)[:, :, 0:P]
            btv = BTs[:, :, 0:P] if T.sample else BT[:, :, 0:P]
            btk = "BTs" if T.sample else "BT"
            S.op("dve", lambda e, psv=psv, tS=tS: e.tensor_tensor(out=tS, in0=psv, in1=bc(lngT, 0, 128, 8, [[1, 8], [0, P]]), op=ALU.mult),
                 reads=[r_bank[b0], r_bank[b0 + 1], rstat("c_lgT")], writes=[r_tmpF])
            S.op("dve", lambda e, tS=tS, btv=btv: e.tensor_tensor(out=tS, in0=tS, in1=btv, op=ALU.add),
                 reads=[r_tmpF, rstat(btk)], writes=[r_tmpF])
            S.op("dve", lambda e, tS=tS, s=s: e.tensor_tensor(out=aT[:, :, s * P:(s + 1) * P], in0=tS, in1=guT[:, :, s * P:(s + 1) * P], op=ALU.mult),
                 reads=[r_tmpF, r_gu], writes=[r_aT])

        NC4 = 4 * P
        bt_pending = []
        gate_ctr = [0]
        gate_slot = [None]
        for s in range(NS):
            kts = T.keytiles(s)
            obase = 0
            sbase = 4
            ob = s % 2
            pv_pending = []
            for g in range(4):
                pb = g % 2
                for (xi, slot, nk, regions) in kts:
                    b = sbase + xi
                    for par, Kt in enumerate((Ke, Ko)):
                        S.op("pe", lambda e, b=b, par=par, Kt=Kt, slot=slot, nk=nk, g=g, s=s: e.matmul(
                            ps[0:nk, b, par * 2 * P:(par + 1) * 2 * P].rearrange("p (a q) -> p a q", q=P),
                            lhsT=Kt[:, g, slot * 128:slot * 128 + nk],
                            rhs=QT[:, 2 * g:2 * g + 2, s * P:(s + 1) * P], start=True, stop=True),
                            reads=[r_K[slot], r_QT], writes=[r_bank[b]], inc=(par == 1))
                    for (r0, r1, q0, q1) in regions:
                        S.op("act", lambda e, b=b, xi=xi, r0=r0, r1=r1, q0=q0, q1=q1, pb=pb: e.activation(
                            out=PT[r0:r1, pb, xi, 0:NC4].rearrange("p (a q) -> p a q", q=P)[:, :, q0:q1],
                            in_=ps[r0:r1, b, 0:NC4].rearrange("p (a q) -> p a q", q=P)[:, :, q0:q1],
                            func=AF.Exp, scale=0.125),
                            reads=[r_bank[b]], writes=[r_PT[pb][xi]])
                for _ in range(4 // NS):
                    gci = gate_ctr[0]
                    gate_ctr[0] += 1
                    bi, epi, fc0 = gate_blocks[gci // 4]
                    fcl = gci % 4
                    if fcl == 0:
                        gate_slot[0] = get_block(bi)
                    gsl = gate_slot[0]
                    gb_ = sbase + 2 + (gci % 2)
                    for kc in range(8):
                        S.op("pe", lambda e, kc=kc, gb_=gb_, fcl=fcl, gsl=gsl: e.matmul(ps[:, gb_, 0:NT], lhsT=ring[:, gsl, kc, fcl * 128:(fcl + 1) * 128],
                                                                                   rhs=nT[:, kc, 0:NT], start=(kc == 0), stop=(kc == 7)),
                             reads=r_nT_all + [r_ring[gsl]], writes=[r_bank[gb_]], inc=(kc == 7))
                    epi(fc0 + fcl, gb_)
                if g == 1 and bt_pending:
                    bt_pending.pop(0)()

                def pv(g=g, pb=pb, kts=kts, s=s, obase=obase):
                    bo = obase + g
                    for sl in range(4):
                        par, pl = sl // 2, sl % 2
                        j = 2 * pl + par
                        for ki, (xi, slot, nk, regions) in enumerate(kts):
                            last = (ki == len(kts) - 1)
                            S.op("pe", lambda e, bo=bo, xi=xi, slot=slot, nk=nk, ki=ki, last=last, sl=sl, j=j: e.matmul(
                                ps[0:P, bo, j * 65:j * 65 + 65], lhsT=PT[0:nk, pb, xi, sl * P:(sl + 1) * P], rhs=Vx[0:nk, slot, g, 0:65],
                                start=(ki == 0), stop=last),
                                reads=[r_V[slot], r_PT[pb][xi]], writes=[r_bank[bo]], inc=(last and sl == 3))
                if pv_pending:
                    pv_pending.pop(0)()
                pv_pending.append(pv)
            while pv_pending:
                pv_pending.pop(0)()
            obanks = [r_bank[obase + i] for i in range(4)]
            dv = bass.AP(ps, obase * 512 + 64, [[4096, P], [512, 4], [65, 4]])
            ov = bass.AP(ps, obase * 512, [[4096, P], [512, 4], [65, 4], [1, 64]])
            ds_ = dstat[0:P, ob * 32:ob * 32 + 16]
            rr_ = dstat[0:P, ob * 32 + 16:ob * 32 + 32]
            S.op("dve", lambda e, dv=dv, ds_=ds_: e.tensor_tensor(out=ds_.rearrange("p (a b) -> p a b", b=4), in0=dv,
                                                                 in1=esB[0:P, :].rearrange("p (a b) -> p a b", b=4), op=ALU.add),
                 reads=obanks + [rstat("esB")], writes=[r_dstat[ob]])
            S.op("dve", lambda e, ds_=ds_, rr_=rr_: e.reciprocal(out=rr_, in_=ds_), reads=[r_dstat[ob]], writes=[r_dstat[ob]])
            S.op("dve", lambda e, ov=ov, rr_=rr_, ob=ob: e.tensor_tensor(
                out=btok[0:P, ob, :].rearrange("p (a b d) -> p a b d", b=4, d=64), in0=ov,
                in1=bc(dstat, ob * 32 + 16, P, 64, [[4, 4], [1, 4], [0, 64]]), op=ALU.mult),
                reads=obanks + [r_dstat[ob]], writes=[r_btok[ob]])

            def bt_part(s=s, ob=ob, obase=obase):
                tb_ = obase + 3
                for hp in range(8):
                    S.op("pe", lambda e, hp=hp: e.transpose(tr_view(tb_)[:, hp, 0:P], btok[0:P, ob, hp * 128:(hp + 1) * 128], ident[0:P, 0:P]),
                         reads=[r_btok[ob], r_const], writes=[r_bank[tb_]], inc=(hp == 7))
                S.op("dve", lambda e: e.tensor_copy(out=bT[:, :, s * P:(s + 1) * P], in_=tr_view(tb_)[:, :, 0:P]),
                     reads=[r_bank[tb_]], writes=[r_gu])
            bt_pending.append(bt_part)
        while bt_pending:
            bt_pending.pop(0)()
        bank_ptr[0] = 0

        if first_tile:
            prepass(range(17, 25))

        for half in range(2):
            sla = get_block(11 + 2 * half)
            slb = get_block(12 + 2 * half)
            bas, bbs = [], []
            for fcl in range(4):
                ba = alloc1()
                bas.append(ba)
                for kc in range(8):
                    S.op("pe", lambda e, kc=kc, ba=ba, fcl=fcl, sla=sla: e.matmul(ps[:, ba, 0:NT], lhsT=ring[:, sla, kc, fcl * 128:(fcl + 1) * 128],
                                                                               rhs=aT[:, kc, 0:NT], start=(kc == 0), stop=(kc == 7)),
                         reads=[r_aT, r_ring[sla]], writes=[r_bank[ba]], inc=(kc == 7))
            for fcl in range(4):
                bbk = alloc1()
                bbs.append(bbk)
                for kc in range(8):
                    S.op("pe", lambda e, kc=kc, bbk=bbk, fcl=fcl, slb=slb: e.matmul(ps[:, bbk, 0:NT], lhsT=ring[:, slb, kc, fcl * 128:(fcl + 1) * 128],
                                                                                 rhs=bT[:, kc, 0:NT], start=(kc == 0), stop=(kc == 7)),
                         reads=[r_gu, r_ring[slb]], writes=[r_bank[bbk]], inc=(kc == 7))
                fc = half * 4 + fcl
                ba = bas[fcl]
                tb = fc % 2
                S.op("dve", lambda e, fc=fc, ba=ba, tb=tb: e.scalar_tensor_tensor(out=t1[:, tb, 0:NT], in0=tg[:, 0, fc, 0:NT], scalar=1.0,
                                                                                 in1=ps[:, ba, 0:NT], op0=ALU.add, op1=ALU.mult),
                     reads=[r_tg[0], r_bank[ba]], writes=[r_t1[tb]])
                S.op("dve", lambda e, fc=fc, bbk=bbk, tb=tb: e.scalar_tensor_tensor(out=relu_t[:, tb, 0:NT], in0=tg[:, 1, fc, 0:NT], scalar=1.0,
                                                                                   in1=ps[:, bbk, 0:NT], op0=ALU.add, op1=ALU.mult),
                     reads=[r_tg[1], r_bank[bbk]], writes=[r_relu[tb]])
                S.op("dve", lambda e, fc=fc, tb=tb: e.tensor_tensor(out=mT[:, fc, 0:NT], in0=t1[:, tb, 0:NT], in1=relu_t[:, tb, 0:NT], op=ALU.add),
                     reads=[r_t1[tb], r_relu[tb]], writes=[r_QT])

        if first_tile:
            prepass(range(25, 33))

        slo = [get_block(15), get_block(16)]
        fb0 = {}

        def f_st1(s):
            b0 = alloc2()
            fb0[s] = b0
            for cb in range(2):
                for kc in range(8):
                    S.op("pe", lambda e, kc=kc, cb=cb, b0=b0, s=s: e.matmul(ps[0:P, b0 + cb, 0:512], lhsT=mT[:, kc, s * P:(s + 1) * P],
                                                                          rhs=ring[:, slo[cb], kc, :], start=(kc == 0), stop=(kc == 7)),
                         reads=[r_QT, r_ring[slo[cb]]], writes=[r_bank[b0 + cb]], inc=(kc == 7))

        def f_st1_epi(s):
            b0 = fb0[s]
            pz = ps[0:P, b0:b0 + 2, :].rearrange("p a c -> p (a c)")
            c_ssq, c_rs = 24 + s, 28 + s
            S.op("act", lambda e, pz=pz, c_ssq=c_ssq: e.activation(out=junk[0:P, :], in_=pz, func=AF.Square, scale=0.5,
                                                                  accum_out=stat[0:P, c_ssq:c_ssq + 1]),
                 reads=[r_bank[b0], r_bank[b0 + 1]], writes=[r_junk, rstat("ssq%d" % c_ssq)])
            rsqrt_chain(stat[0:P, c_ssq:c_ssq + 1], stat[0:P, c_rs:c_rs + 1], P, 1, 1.0 / D,
                        rstat("ssq%d" % c_ssq), rstat("rs%d" % c_rs), "rc%d" % c_rs)

        def f_st2(s):
            xsl = T.xslots[s]
            b0 = fb0[s]
            pz = ps[0:P, b0:b0 + 2, :].rearrange("p a c -> p (a c)")
            c_rs = 28 + s
            tbuf, tres = (tmpF, [r_tmpF]) if s % 2 == 0 else (gv, [r_gv[0], r_gv[1]])
            S.op("dve", lambda e, pz=pz, c_rs=c_rs, tbuf=tbuf: e.scalar_tensor_tensor(out=tbuf[0:P, :], in0=pz, scalar=stat[0:P, c_rs:c_rs + 1], in1=g2B[0:P, :],
                                                                                     op0=ALU.mult, op1=ALU.mult),
                 reads=[r_bank[b0], r_bank[b0 + 1], rstat("rs%d" % c_rs), rstat("g2B")], writes=tres)
            S.op("dve", lambda e, xsl=xsl, tbuf=tbuf: e.tensor_tensor(out=X[0:P, xsl, :], in0=X[0:P, xsl, :], in1=tbuf[0:P, :], op=ALU.add),
                 reads=tres + [r_X[xsl]], writes=[r_X[xsl]])

        fstg = [norm_stages(T, s_, T.xslots[s_], g3T, "c_g3T", 32 + s_, 36 + s_) for s_ in range(NS)]
        for k in range(NS + 3):
            if k < NS:
                f_st1(k)
            if 0 <= k - 1 < NS:
                f_st2(k - 1)
            if k < NS:
                f_st1_epi(k)
            if 0 <= k - 3 < NS:
                fstg[k - 3][2]()
            if 0 <= k - 1 < NS:
                fstg[k - 1][0]()
            if 0 <= k - 2 < NS:
                fstg[k - 2][1]()

        def h_epi(fc, b):
            tb = fc % 2
            S.op("act", lambda e: e.activation(out=relu_t[:, tb, 0:NT], in_=ps[:, b, 0:NT], func=AF.Relu),
                 reads=[r_bank[b]], writes=[r_relu[tb]])
            S.op("dve", lambda e: e.tensor_tensor(out=hT[:, fc, 0:NT], in0=relu_t[:, tb, 0:NT], in1=relu_t[:, tb, 0:NT], op=ALU.mult),
                 reads=[r_relu[tb]], writes=[r_hT, r_gu, r_QT, r_aT] + r_vn)

        hoist = []
        for j in range(8):
            if j == 5 and Tnext is not None:
                hs = [s_ for s_ in range(Tnext.NS)]
                hoist = [phaseA_stages(Tnext, s_) for s_ in hs]
                Tnext.a_done = set(hs)
                for st in hoist:
                    st[0]()
                for st in hoist[0:3]:
                    st[1]()
            sl = get_block(17 + j)
            feat_block(sl, nT, r_nT_all, h_epi, j * 4)

        hbanks = [[None] * NS for _ in range(2)]
        for st in hoist[0:3]:
            st[2]()
        for st in hoist[3:]:
            st[1]()
            st[2]()
        for ch in range(2):
            for s in range(NS):
                hbanks[ch][s] = alloc1()
            for pc in range(4):
                sl = get_block(25 + ch * 4 + pc)
                for s in range(NS):
                    b = hbanks[ch][s]
                    for kcl in range(8):
                        kc = pc * 8 + kcl
                        S.op("pe", lambda e, kc=kc, kcl=kcl, b=b, s=s, sl=sl: e.matmul(ps[0:P, b, 0:512], lhsT=hT[:, kc, s * P:(s + 1) * P],
                                                                                     rhs=ring[:, sl, kcl, :], start=(kc == 0), stop=(kc == 31)),
                             reads=[r_hT, r_gu, r_QT, r_aT] + r_vn + [r_ring[sl]], writes=[r_bank[b]], inc=(kcl == 7))
            for s in range(NS):
                b = hbanks[ch][s]
                col = 44 + ch
                S.op("act", lambda e, b=b, s=s, ch=ch: e.activation(out=junk[0:P, 0:512], in_=ps[0:P, b, 0:512], func=AF.Square,
                                                                  accum_out=stat[0:P, 52 + ch * 4 + s:53 + ch * 4 + s]),
                     reads=[r_bank[b]], writes=[r_junk, rstat("hss%d_%d" % (ch, s))])
        for s in range(NS):
            S.op("dve", lambda e, s=s: e.tensor_tensor(out=stat[0:P, 52 + s:53 + s], in0=stat[0:P, 52 + s:53 + s], in1=stat[0:P, 56 + s:57 + s], op=ALU.add),
                 reads=[rstat("hss0_%d" % s), rstat("hss1_%d" % s)], writes=[rstat("hss0_%d" % s)])
            rsqrt_chain(stat[0:P, 52 + s:53 + s], stat[0:P, 56 + s:57 + s], P, 1, 1.0 / D,
                        rstat("hss0_%d" % s), rstat("hss1_%d" % s), "hrc%d" % s)
        def h_scale(s):
            tbuf, tres = (tmpF, [r_tmpF]) if s % 2 == 0 else (gv, [r_gv[0], r_gv[1]])
            for ch in range(2):
                b = hbanks[ch][s]
                S.op("dve", lambda e, b=b, s=s, ch=ch, tbuf=tbuf: e.scalar_tensor_tensor(out=tbuf[0:P, ch * 512:(ch + 1) * 512], in0=ps[0:P, b, 0:512],
                                                                                        scalar=stat[0:P, 56 + s:57 + s], in1=g4B[0:P, ch * 512:(ch + 1) * 512],
                                                                                        op0=ALU.mult, op1=ALU.mult),
                     reads=[r_bank[b], rstat("hss1_%d" % s), rstat("g4B")], writes=tres)

        def h_add(s):
            xsl = T.xslots[s]
            tbuf, tres = (tmpF, [r_tmpF]) if s % 2 == 0 else (gv, [r_gv[0], r_gv[1]])
            S.op("dve", lambda e, xsl=xsl, tbuf=tbuf: e.tensor_tensor(out=X[0:P, xsl, :], in0=X[0:P, xsl, :], in1=tbuf[0:P, :], op=ALU.add),
                 reads=tres + [r_X[xsl]], writes=[r_X[xsl]])
            dst = T.y_dst(s)
            S.dma("pool", "xst%d" % xsl, lambda e, dst=dst, xsl=xsl: e.dma_start(out=dst, in_=X[0:P, xsl, :]), reads=[r_X[xsl]])
            if ("xst%d" % xsl) not in out_sems:
                out_sems.append("xst%d" % xsl)

        for s in range(NS + 1):
            if s < NS:
                h_scale(s)
            if s >= 1:
                h_add(s - 1)

    def load_x(T, s_list):
        for s in s_list:
            xsl = T.xslots[s]
            src = T.x_src(s)
            S.dma("pool", "xld%d" % xsl, lambda e, src=src, xsl=xsl, P=T.P: e.dma_start(out=X[0:P, xsl, :], in_=src), writes=[r_X[xsl]])

    tiles = []
    for ti in range(n_ptiles):
        T = Tile()
        T.a_done = set()
        T.sample = False
        T.P, T.NS, T.NT = 128, 4, 512
        seq, tt = ti // 8, ti % 8
        T.seq, T.tt = seq, tt
        T.xslots = [(xslot_ctr[0] + s) % NXS for s in range(4)]
        xslot_ctr[0] += 4
        T.rope_blk = (lambda s, tt=tt: tt * 4 + s)
        T.x_src = (lambda s, seq=seq, tt=tt: xp[seq, tt * 512 + s * 128: tt * 512 + (s + 1) * 128, :])
        T.y_dst = (lambda s, seq=seq, tt=tt: yp[seq, tt * 512 + s * 128: tt * 512 + (s + 1) * 128, :])
        T.kslot_new = (lambda s, tt=tt: (tt * 4 + s) % NKS)

        def keytiles(s, tt=tt):
            gs = tt * 4 + s
            res = []
            if gs > 0:
                res.append((0, (gs - 1) % NKS, 128, [(0, 128, 0, 64), (64, 128, 64, 128)]))
            res.append((1, gs % NKS, 128, [(0, 64, 0, 64), (0, 128, 64, 128)]))
            return res
        T.keytiles = keytiles

        def state_out(s, seq=seq, tt=tt):
            if tt == 7 and s == 3:
                return (kp[seq, :, :], vp[seq, :, :], "o_p%d" % seq)
            return None
        T.state_out = state_out
        tiles.append(T)

    if do_sample:
        T = Tile()
        T.a_done = set()
        T.sample = True
        T.P, T.NS, T.NT = 16, 2, 32
        T.xslots = [(xslot_ctr[0] + s) % NXS for s in range(2)]
        T.lg_slot = (xslot_ctr[0] + 2) % NXS
        xslot_ctr[0] += 2
        T.rope_blk = (lambda s: 32)
        T.x_src = (lambda s: xsm[s, :, :])
        T.y_dst = (lambda s: ys[s, :, :])
        T.kslot_new = (lambda s: 2 * s + 1)
        T.kslot_cache = (lambda s: 2 * s)
        T.keytiles = (lambda s: [(0, 2 * s, 128, [(0, 128, 0, 16)]), (1, 2 * s + 1, 16, [(0, 16, 0, 16)])])
        T.state_out = (lambda s: (ksn[s, :, :], vsn[s, :, :], "o_s%d" % s))
        tiles.append(T)

    for i, T in enumerate(tiles):
        if i == 0:
            load_x(T, range(T.NS))
        if i + 1 < len(tiles):
            load_x(tiles[i + 1], range(tiles[i + 1].NS))
        process_tile(T, tiles[i + 1] if i + 1 < len(tiles) else None)

    if do_sample:
        out_sems.append("o_vsg")
    S.wait_all("pool", [k for k in dict.fromkeys(out_sems)])

    stuck = S.check_deadlock()
    if stuck:
        raise RuntimeError("static deadlock detected: %r" % (stuck,))
    with nc.Block() as block:
        @block.tensor
        def _(e):
            for f in S.prog["pe"]:
                f(e)

        @block.scalar
        def _(e):
            for f in S.prog["act"]:
                f(e)

        @block.vector
        def _(e):
            for f in S.prog["dve"]:
                f(e)

        @block.gpsimd
        def _(e):
            for f in S.prog["pool"]:
                f(e)

        @block.sync
        def _(e):
            for f in S.prog["sp"]:
                f(e)
    return nc


def _rope_table():
    half = 8
    inv = (np.float32(500000.0) ** (-np.arange(half, dtype=np.float32) * np.float32(2.0) / np.float32(16))).astype(np.float32)
    tab = np.zeros((128, 33, 32), np.float32)
    p = np.arange(128)
    for blk in range(33):
        if blk < 32:
            pos = (blk * 128 + p).astype(np.float32)
        else:
            pos = (4096 + (p % 16)).astype(np.float32)
        ang = (pos[:, None] * inv[None, :]).astype(np.float32)
        c = np.cos(ang).astype(np.float32)
        s = np.sin(ang).astype(np.float32)
        tab[:, blk, 0:8] = c
        tab[:, blk, 8:16] = c
        tab[:, blk, 16:24] = -s
        tab[:, blk, 24:32] = s
    return tab


_NC_CACHE = {}


def kernel(x_prompt, x_sample, cache_swa_k, cache_swa_v, w_in, sgu_ln_g, sgu_ln_b, sgu_w, sgu_b, attn_sinks,
           w_branch_a, w_branch_b, w_out, g_mix_pre, g_mix_post, g_ffn_pre, g_ffn_post, w_ff1, w_ff2,
           _n_ptiles=16, _do_sample=True):
    f = lambda a: np.ascontiguousarray(np.asarray(a, dtype=np.float32))
    x_prompt = np.asarray(x_prompt, dtype=np.float32)
    x_sample = f(x_sample)
    ckf = f(cache_swa_k)[0].reshape(16, 128, 256)
    cvf = f(cache_swa_v)[0].reshape(16, 128, 256)
    shared = {
        "w_in": f(w_in)[0], "wa": f(w_branch_a)[0], "wb": f(w_branch_b)[0], "wo": f(w_out)[0],
        "w1": f(w_ff1)[0], "w2": f(w_ff2)[0], "sgu_w": f(sgu_w)[0], "sgu_b": f(sgu_b)[0].reshape(1, 1024),
        "ln_g": f(sgu_ln_g)[0].reshape(1, 1024), "ln_b": f(sgu_ln_b)[0].reshape(1, 1024),
        "sinks": f(attn_sinks)[0].reshape(1, 16),
        "g1": f(g_mix_pre)[0].reshape(1, 1024), "g2": f(g_mix_post)[0].reshape(1, 1024),
        "g3": f(g_ffn_pre)[0].reshape(1, 1024), "g4": f(g_ffn_post)[0].reshape(1, 1024),
        "rope": _rope_table(), "ident": np.eye(128, dtype=np.float32),
    }
    in_maps = []
    for c in range(NCORES):
        m = dict(shared)
        m["xp"] = np.ascontiguousarray(x_prompt[2 * c:2 * c + 2])
        m["xs"] = np.ascontiguousarray(x_sample[2 * c:2 * c + 2])
        m["ck"] = np.ascontiguousarray(ckf[2 * c:2 * c + 2])
        m["cv"] = np.ascontiguousarray(cvf[2 * c:2 * c + 2])
        in_maps.append(m)
    key = (_n_ptiles, _do_sample)
    if key not in _NC_CACHE:
        _NC_CACHE[key] = build_program(_n_ptiles, _do_sample)
    nc = _NC_CACHE[key]
    res = run_bass_kernel_spmd(nc, in_maps, core_ids=list(range(NCORES)))
    R = res.results
    y_prompt = np.concatenate([r["yp"] for r in R], axis=0)
    y_sample = np.concatenate([r["ys"] for r in R], axis=0)
    kpo = np.concatenate([r["kp"] for r in R], axis=0).reshape(1, 16, 128, 4, 64)
    vpo = np.concatenate([r["vp"] for r in R], axis=0).reshape(1, 16, 128, 4, 64)
    kso = np.concatenate([r["ksn"] for r in R], axis=0).reshape(1, 16, 16, 4, 64)
    vso = np.concatenate([r["vsn"] for r in R], axis=0).reshape(1, 16, 16, 4, 64)
    vsgo = np.concatenate([r["vsg"] for r in R], axis=0).reshape(1, 16, 16, 1024)
    return (y_prompt, y_sample, kpo, vpo, kso, vso, vsgo)
```

```python
import numpy as np
import concourse.bass as bass
import concourse.mybir as mybir
from concourse.bass_utils import run_bass_kernel_spmd

F32 = mybir.dt.float32
BF16 = mybir.dt.bfloat16
AF = mybir.ActivationFunctionType
ALU = mybir.AluOpType

NCORES = 8
SEQ = 4096
D = 1024
EPS = 1e-6
NSLOT = 5
NXS = 8
NKS = 6
NBLK = 33
SAME_ENGINE_WAW = False


class Res:
    __slots__ = ("w", "r", "name")

    def __init__(self, name=""):
        self.w = {}
        self.r = {}
        self.name = name


class Sched:
    ENG = ("pe", "act", "dve", "pool", "sp")

    def __init__(self, nc):
        self.nc = nc
        self.prog = {e: [] for e in self.ENG}
        self.abs = {e: [] for e in self.ENG}
        self.sem = {}
        self.cnt = {}
        self.waited = {e: {} for e in self.ENG}
        for e in ("pe", "act", "dve", "pool"):
            self.newsem(e)

    def newsem(self, key):
        self.sem[key] = self.nc.alloc_semaphore("s_" + key)
        self.cnt[key] = 0

    def _deps(self, eng, reads, writes):
        deps = {}
        for r in reads:
            for k, v in r.w.items():
                if k == eng and eng == "pe":
                    continue
                if deps.get(k, 0) < v:
                    deps[k] = v
        for w in writes:
            for k, v in w.w.items():
                if k == eng and (eng == "pe" or not SAME_ENGINE_WAW):
                    continue
                if deps.get(k, 0) < v:
                    deps[k] = v
            for k, v in w.r.items():
                if k == eng and (eng == "pe" or not SAME_ENGINE_WAW):
                    continue
                if deps.get(k, 0) < v:
                    deps[k] = v
        return deps

    def _emit_waits(self, eng, deps):
        wd = self.waited[eng]
        for k, v in deps.items():
            if wd.get(k, 0) < v:
                wd[k] = v
                h = self.sem[k]
                self.prog[eng].append(lambda e, h=h, v=v: e.wait_ge(h, v))
                self.abs[eng].append(("wait", k, v))

    def op(self, eng, fn, reads=(), writes=(), inc=True):
        deps = self._deps(eng, reads, writes)
        self._emit_waits(eng, deps)
        if inc:
            self.cnt[eng] += 1
            v = self.cnt[eng]
            h = self.sem[eng]
            self.prog[eng].append(lambda e, fn=fn, h=h: fn(e).then_inc(h, 1))
            self.abs[eng].append(("inc", eng, 1))
        else:
            v = self.cnt[eng] + 1
            self.prog[eng].append(lambda e, fn=fn: fn(e))
        for w in writes:
            w.w = {eng: v}
            w.r = {}
        for r in reads:
            if r.r.get(eng, 0) < v:
                r.r[eng] = v

    def dma(self, qeng, semkey, fn, reads=(), writes=()):
        if semkey not in self.sem:
            self.newsem(semkey)
        deps = self._deps(None, reads, writes)
        self._emit_waits(qeng, deps)
        self.cnt[semkey] += 16
        v = self.cnt[semkey]
        h = self.sem[semkey]
        self.prog[qeng].append(lambda e, fn=fn, h=h: fn(e).then_inc(h, 16))
        self.abs[qeng].append(("inc", semkey, 16))
        for w in writes:
            w.w = {semkey: v}
            w.r = {}
        for r in reads:
            if r.r.get(semkey, 0) < v:
                r.r[semkey] = v

    def check_deadlock(self):
        val = {k: 0 for k in self.sem}
        pc = {e: 0 for e in self.ENG}
        progress = True
        while progress:
            progress = False
            for e in self.ENG:
                lst = self.abs[e]
                while pc[e] < len(lst):
                    kind, k, v = lst[pc[e]]
                    if kind == "wait":
                        if val[k] >= v:
                            pc[e] += 1
                            progress = True
                        else:
                            break
                    else:
                        val[k] += v
                        pc[e] += 1
                        progress = True
        stuck = {e: (pc[e], len(self.abs[e]), self.abs[e][pc[e]] if pc[e] < len(self.abs[e]) else None, val.get(self.abs[e][pc[e]][1]) if pc[e] < len(self.abs[e]) else None)
                 for e in self.ENG if pc[e] < len(self.abs[e])}
        return stuck

    def wait_all(self, eng, keys):
        deps = {k: self.cnt[k] for k in keys if self.cnt[k] > 0}
        self._emit_waits(eng, deps)


def build_program(n_ptiles=16, do_sample=True):
    nc = bass.Bass("TRN2", target_bir_lowering=False)
    S = Sched(nc)

    def din(name, shape):
        return nc.dram_tensor(name, list(shape), F32, kind="ExternalInput").ap()

    def dout(name, shape):
        return nc.dram_tensor(name, list(shape), F32, kind="ExternalOutput").ap()

    xp = din("xp", [2, SEQ, D])
    xsm = din("xs", [2, 16, D])
    ck = din("ck", [2, 128, 256])
    cv = din("cv", [2, 128, 256])
    w_in = din("w_in", [D, 5632])
    wa = din("wa", [D, D])
    wb = din("wb", [D, D])
    wo = din("wo", [D, D])
    w1 = din("w1", [D, 4096])
    w2 = din("w2", [4096, D])
    sgu_w = din("sgu_w", [8, 128, 128])
    sgu_b = din("sgu_b", [1, 1024])
    ln_g = din("ln_g", [1, 1024])
    ln_b = din("ln_b", [1, 1024])
    sinks = din("sinks", [1, 16])
    g1 = din("g1", [1, 1024])
    g2 = din("g2", [1, 1024])
    g3 = din("g3", [1, 1024])
    g4 = din("g4", [1, 1024])
    rope_d = din("rope", [128, 33, 32])
    ident_d = din("ident", [128, 128])

    yp = dout("yp", [2, SEQ, D])
    ys = dout("ys", [2, 16, D])
    kp = dout("kp", [2, 128, 256])
    vp = dout("vp", [2, 128, 256])
    ksn = dout("ksn", [2, 16, 256])
    vsn = dout("vsn", [2, 16, 256])
    vsg = dout("vsg", [2, 16, D])

    wscr = nc.dram_tensor("wscr", [NBLK, 128, 8, 512], BF16).ap()

    def sb(name, shape, dt):
        return nc.alloc_sbuf_tensor("sb_" + name, list(shape), dt)

    ident = sb("ident", [128, 128], BF16)
    WsT = sb("WsT", [128, 8, 128], BF16)
    BT = sb("BT", [128, 8, 128], F32)
    BTs = sb("BTs", [128, 8, 16], F32)
    lngT = sb("lngT", [128, 8], F32)
    g1T = sb("g1T", [128, 8], F32)
    g3T = sb("g3T", [128, 8], F32)
    g2B = sb("g2B", [128, 1024], F32)
    g4B = sb("g4B", [128, 1024], F32)
    rope = sb("rope", [128, 33, 32], F32)
    mhalf = sb("mhalf", [128, 8], F32)
    stat = sb("stat", [128, 64], F32)

    X = sb("X", [128, NXS, 1024], F32)
    nT = sb("nT", [128, 8, 512], BF16)
    tg = sb("tg", [128, 2, 8, 512], BF16)
    R32 = sb("R32", [128, 16384], BF16)
    qtok = sb("qtok", [128, 3, 1024], BF16)
    kf = sb("kf", [128, 256], F32)
    vf = sb("vf", [128, 256], F32)
    kb = sb("kb", [128, 4, 4, 128], BF16)
    Ke = sb("Ke", [128, 4, NKS * 128], BF16)
    Ko = sb("Ko", [128, 4, NKS * 128], BF16)
    Vx = sb("Vx", [128, NKS, 4, 128], BF16)
    PT = sb("PT", [128, 2, 2, 512], BF16)
    xsb = sb("xsb", [128, 3, 1024], BF16)
    junk = sb("junk", [128, 1024], BF16)
    btok = sb("btok", [128, 2, 1024], BF16)
    dstat = sb("dstat", [128, 64], F32)
    esB = sb("esB", [128, 16], F32)
    t1 = sb("t1", [128, 2, 512], F32)
    tmpF = sb("tmpF", [128, 1024], F32)
    relu_t = sb("relu_t", [128, 2, 512], F32)
    cA = t1[:, :, :].rearrange("p a c -> p (a c)")
    gv = cA
    cB = relu_t[:, :, :].rearrange("p a c -> p (a c)")
    rtmp = sb("rtmp", [128, 2, 8, 16], F32)
    ring = sb("ring", [128, NSLOT, 8, 512], BF16)
    ps = nc.alloc_psum_tensor("ps", [128, 8, 512], F32)
    print("SBUF bytes remaining per partition:", nc.sbuf_bytes_remaining, flush=True)

    hT = R32[:, :].rearrange("p (a t) -> p a t", t=512)
    guT = R32[:, 0:4096].rearrange("p (a t) -> p a t", t=512)
    vn = R32[:, 4096:8192].rearrange("p (a t) -> p a t", t=1024)
    QT = R32[:, 8192:12288].rearrange("p (a t) -> p a t", t=512)
    aT = R32[:, 12288:16384].rearrange("p (a t) -> p a t", t=512)
    bT = guT
    mT = QT

    r_const = Res("const")
    r_X = [Res("X%d" % i) for i in range(NXS)]
    r_nT = [Res("nT%d" % i) for i in range(4)]
    r_tg = [Res("tga"), Res("tgb")]
    r_gu = Res("guT/bT")
    r_vn = [Res("vn%d" % i) for i in range(4)]
    r_QT = Res("QT/mT")
    r_aT = Res("aT")
    r_hT = Res("hT")
    r_qtok = [[Res("qrope"), Res("qrest")] for _ in range(3)]
    r_kf = [Res("kfrope"), Res("kfrest")]
    r_vf = Res("vf")
    r_kb = [Res("kb%d" % i) for i in range(4)]
    r_K = [Res("K%d" % i) for i in range(NKS)]
    r_V = [Res("V%d" % i) for i in range(NKS)]
    r_PT = [[Res("PTA"), Res("PTB")] for _ in range(2)]
    r_xsb = [Res("xsb0"), Res("xsb1"), Res("xsb2")]
    r_junk = Res("junk")
    r_btok = [Res("btok0"), Res("btok1")]
    r_dstat = [Res("dstat0"), Res("dstat1")]
    r_t1 = [Res("t10"), Res("t11")]
    r_gv = r_t1
    r_rscr = [Res("rscr0"), Res("rscr1")]
    r_rsq = [Res("rsq%d" % i) for i in range(8)]
    rsq_ctr = [0]
    r_tmpF = Res("tmpF")
    r_relu = [Res("relu0"), Res("relu1")]
    r_rtmp = [Res("rtmp0"), Res("rtmp1")]
    r_ring = [Res("ring%d" % i) for i in range(NSLOT)]
    r_bank = [Res("bank%d" % i) for i in range(8)]
    r_stat = {}
    r_blk = [Res("scr%d" % i) for i in range(NBLK)]

    def rstat(name):
        if name not in r_stat:
            r_stat[name] = Res(name)
        return r_stat[name]

    out_sems = []

    bank_ptr = [0]

    def alloc1():
        b = bank_ptr[0]
        bank_ptr[0] = (b + 1) % 8
        return b

    def alloc2():
        if bank_ptr[0] % 2:
            bank_ptr[0] = (bank_ptr[0] + 1) % 8
        b = bank_ptr[0]
        bank_ptr[0] = (b + 2) % 8
        return b

    def tr_view(b):
        return ps[:, b, :].bitcast(BF16).rearrange("p (a t) -> p a t", t=128)

    def bc(t, off, nparts, pstep, dims):
        return bass.AP(t, off, [[pstep, nparts]] + [list(d) for d in dims])

    w_in_v = w_in.rearrange("(kc p) c -> p kc c", p=128)
    wa_v = wa.rearrange("(kc p) c -> p kc c", p=128)
    wb_v = wb.rearrange("(kc p) c -> p kc c", p=128)
    wo_v = wo.rearrange("(kc p) c -> p kc c", p=128)
    w1_v = w1.rearrange("(kc p) c -> p kc c", p=128)
    w2_v = w2.rearrange("(kc p) c -> p kc c", p=128)
    blk_src = []
    for c0 in (1024, 1536, 2048, 2560, 3072, 0, 512, 3584, 4096, 4608, 5120):
        blk_src.append(w_in_v[:, :, c0:c0 + 512])
    blk_src.append(wa_v[:, :, 0:512])
    blk_src.append(wb_v[:, :, 0:512])
    blk_src.append(wa_v[:, :, 512:1024])
    blk_src.append(wb_v[:, :, 512:1024])
    blk_src.append(wo_v[:, :, 0:512])
    blk_src.append(wo_v[:, :, 512:1024])
    for j in range(8):
        blk_src.append(w1_v[:, :, j * 512:(j + 1) * 512])
    for ch in range(2):
        for pc in range(4):
            blk_src.append(w2_v[:, pc * 8:(pc + 1) * 8, ch * 512:(ch + 1) * 512])
    assert len(blk_src) == NBLK

    prepass_done = [False] * NBLK

    def prepass(blist):
        for b in blist:
            if prepass_done[b]:
                continue
            prepass_done[b] = True
            S.dma("pool", "pp%d" % b,
                  lambda e, b=b: e.dma_start(out=wscr[b], in_=blk_src[b]),
                  writes=[r_blk[b]])

    ring_ctr = [0]

    def get_block(b):
        slot = ring_ctr[0] % NSLOT
        ring_ctr[0] += 1
        S.dma("sp", "ring%d" % slot,
              lambda e, b=b, slot=slot: e.dma_start(out=ring[:, slot, :, :], in_=wscr[b]),
              reads=[r_blk[b]], writes=[r_ring[slot]])
        return slot

    prepass(range(0, 11))

    def setup():
        S.dma("pool", "c_id", lambda e: e.dma_start(out=ident[:, :], in_=ident_d), writes=[r_const])
        S.dma("pool", "c_rope", lambda e: e.dma_start(out=rope[:, :, :], in_=rope_d), writes=[rstat("rope")])
        wnat = R32[:, 0:1024].rearrange("p (g j) -> p g j", j=128)
        S.dma("pool", "c_w", lambda e: e.dma_start(out=wnat, in_=sgu_w.rearrange("g i j -> i g j")),
              writes=[r_hT])
        b = alloc1()
        for g in range(8):
            S.op("pe", lambda e, g=g, b=b: e.transpose(tr_view(b)[:, g, :], wnat[:, g, :], ident[:, :]),
                 reads=[r_hT, r_const], writes=[r_bank[b]], inc=(g == 7))
        S.op("dve", lambda e, b=b: e.tensor_copy(out=WsT[:, :, :], in_=tr_view(b)),
             reads=[r_bank[b]], writes=[rstat("WsT")])
        S.op("dve", lambda e: e.memset(WsT[64:128, :, 0:64], 0.0), writes=[rstat("WsT")])
        def bload(key, dst, src, res):
            S.dma("pool", key, lambda e: e.dma_start(out=dst, in_=src.to_broadcast([128, 1024])),
                  writes=(res if isinstance(res, list) else [res]))
        bload("c_bs", cA[:, :], sgu_b, r_t1)
        bload("c_lb", cB[:, :], ln_b, r_relu)
        bload("c_g2", g2B[:, :], g2, rstat("g2B"))
        bload("c_g4", g4B[:, :], g4, rstat("g4B"))
        S.op("dve", lambda e: e.tensor_scalar(out=g2B[:, :], in0=g2B[:, :], scalar1=0.5, scalar2=None, op0=ALU.mult),
             reads=[rstat("g2B")], writes=[rstat("g2B")])
        LBb = R32[:, 1024:2048]
        r_lbb = Res("LBb")
        S.op("dve", lambda e: e.tensor_copy(out=LBb, in_=cB[:, :]), reads=r_relu, writes=[r_lbb])
        b0 = alloc2()
        for g in range(8):
            S.op("pe", lambda e, g=g, b0=b0: e.matmul(ps[:, b0 + g // 4, (g % 4) * 128:(g % 4) * 128 + 128],
                                                    lhsT=LBb[:, g * 128:(g + 1) * 128], rhs=WsT[:, g, :],
                                                    start=True, stop=True),
                 reads=[r_lbb, rstat("WsT")], writes=[r_bank[b0], r_bank[b0 + 1]], inc=(g == 7))
        psv = ps[:, b0:b0 + 2, :].rearrange("p a (g i) -> p (a g) i", i=128)
        S.op("dve", lambda e: e.tensor_tensor(out=BT[:, :, :], in0=psv,
                                              in1=cA[:, :].rearrange("p (g i) -> p g i", i=128), op=ALU.add),
             reads=[r_bank[b0], r_bank[b0 + 1]] + r_t1, writes=[rstat("BT")])
        b1 = alloc1()
        for g in range(8):
            S.op("pe", lambda e, g=g, b1=b1: e.matmul(ps[:, b1, g * 16:(g + 1) * 16],
                                                    lhsT=LBb[0:16, g * 128:(g + 1) * 128], rhs=WsT[0:16, g, 0:16],
                                                    start=True, stop=True),
                 reads=[r_lbb, rstat("WsT")], writes=[r_bank[b1]], inc=(g == 7))
        S.op("dve", lambda e: e.tensor_tensor(out=BTs[:, :, :],
                                              in0=ps[:, b1, 0:128].rearrange("p (g i) -> p g i", i=16),
                                              in1=cA[:, :].rearrange("p (g i) -> p g i", i=128)[:, :, 0:16], op=ALU.add),
             reads=[r_bank[b1]] + r_t1, writes=[rstat("BTs")])
        for key, dst, src in (("c_lgT", lngT, ln_g), ("c_g1T", g1T, g1), ("c_g3T", g3T, g3)):
            S.dma("pool", key, lambda e, dst=dst, src=src: e.dma_start(
                out=dst[:, :], in_=src.rearrange("o (g c) -> (o c) g", c=128), allow_slow_non_contiguous=True),
                writes=[rstat(key)])
        S.dma("pool", "c_sk", lambda e: e.dma_start(out=esB[:, :], in_=sinks.to_broadcast([128, 16])), writes=[rstat("esB")])
        S.op("act", lambda e: e.activation(out=esB[:, :], in_=esB[:, :], func=AF.Exp), reads=[rstat("esB")], writes=[rstat("esB")])
        S.op("dve", lambda e: e.memset(mhalf[:, :], -0.5), writes=[rstat("mhalf")])
        S.op("dve", lambda e: e.memset(Ke[64:128, :, :], 0.0), writes=r_K)
        S.op("dve", lambda e: e.memset(Ko[0:64, :, :], 0.0), writes=r_K)
        S.op("dve", lambda e: e.memset(Vx[:, :, :, 64:128], 1.0), writes=r_V)
        S.op("dve", lambda e: e.memset(PT[:, :, :, :], 0.0), writes=[r_PT[0][0], r_PT[0][1], r_PT[1][0], r_PT[1][1]])

    setup()

    xslot_ctr = [0]
    gtile = [0]

    class Tile:
        pass

    def rsqrt_chain(src_ap, dst_ap, nparts, ncols, scale, rsrc, rdst, key):
        ti = rsq_ctr[0] % 8
        rsq_ctr[0] += 1
        tkey = r_rsq[ti]
        tcol = stat[0:nparts, 41 + ti:41 + ti + ncols]
        S.op("dve", lambda e: e.tensor_scalar(out=tcol, in0=src_ap, scalar1=scale, scalar2=EPS, op0=ALU.mult, op1=ALU.add),
             reads=[rsrc], writes=[tkey])
        S.op("pool", lambda e: e.tensor_tensor(out=dst_ap, in0=tcol, in1=mhalf[0:nparts, 0:ncols], op=ALU.pow),
             reads=[tkey, rstat("mhalf")], writes=[rdst])

    def norm_stages(T, s, xsl, gT, gkey, ssq_col, rs_col):
        P = T.P
        xb = s % 3
        Xs = X[0:P, xsl, :]

        def st1():
            S.op("act", lambda e: e.activation(out=junk[0:P, :], in_=Xs, func=AF.Square, accum_out=stat[0:P, ssq_col:ssq_col + 1]),
                 reads=[r_X[xsl]], writes=[r_junk, rstat("ssq%d" % ssq_col)])
            rsqrt_chain(stat[0:P, ssq_col:ssq_col + 1], stat[0:P, rs_col:rs_col + 1], P, 1, 1.0 / D,
                        rstat("ssq%d" % ssq_col), rstat("rs%d" % rs_col), "rc%d" % rs_col)

        def st2a():
            S.op("act", lambda e: e.activation(out=xsb[0:P, xb, :], in_=Xs, func=AF.Copy, scale=stat[0:P, rs_col:rs_col + 1]),
                 reads=[r_X[xsl], rstat("rs%d" % rs_col)], writes=[r_xsb[xb]])

        def st2b():
            b = alloc1()
            for kc in range(8):
                S.op("pe", lambda e, kc=kc, b=b: e.transpose(tr_view(b)[:, kc, 0:P], xsb[0:P, xb, kc * 128:(kc + 1) * 128],
                                                            ident[0:P, 0:P]),
                     reads=[r_xsb[xb], r_const], writes=[r_bank[b]], inc=(kc == 7))
            S.op("dve", lambda e, b=b: e.tensor_tensor(out=nT[:, :, s * P:(s + 1) * P], in0=tr_view(b)[:, :, 0:P],
                                                      in1=bc(gT, 0, 128, 8, [[1, 8], [0, P]]), op=ALU.mult),
                 reads=[r_bank[b], rstat(gkey)], writes=[r_nT[s]])
        return st1, st2a, st2b

    def phaseA_stages(T, s):
        return norm_stages(T, s, T.xslots[s], g1T, "c_g1T", 16 + s, 20 + s)

    def rope_ops(T, s, src3, nh, dst3, r_src, r_dst_rope, r_dst_rest, rbuf):
        P = T.P
        blk = T.rope_blk(s)
        cc = bc(rope, blk * 32, P, 33 * 32, [[0, nh], [1, 16]])
        nsin = bc(rope, blk * 32 + 16, P, 33 * 32, [[0, nh], [1, 8]])
        psin = bc(rope, blk * 32 + 24, P, 33 * 32, [[0, nh], [1, 8]])
        A = rtmp[0:P, 0, 0:nh, :]
        B = rtmp[0:P, 1, 0:nh, :]
        S.op("dve", lambda e: e.tensor_tensor(out=A, in0=src3[:, :, 0:16], in1=cc, op=ALU.mult),
             reads=[r_src, rstat("rope")], writes=[r_rtmp[0]])
        S.op("dve", lambda e: e.tensor_tensor(out=B[:, :, 0:8], in0=src3[:, :, 8:16], in1=nsin, op=ALU.mult),
             reads=[r_src, rstat("rope")], writes=[r_rtmp[1]])
        S.op("dve", lambda e: e.tensor_tensor(out=B[:, :, 8:16], in0=src3[:, :, 0:8], in1=psin, op=ALU.mult),
             reads=[r_src, rstat("rope")], writes=[r_rtmp[1]])
        S.op("dve", lambda e: e.tensor_tensor(out=dst3[:, :, 0:16], in0=A, in1=B, op=ALU.add),
             reads=[r_rtmp[0], r_rtmp[1]], writes=[r_dst_rope])
        S.op("act", lambda e: e.activation(out=dst3[:, :, 16:64], in_=src3[:, :, 16:64], func=AF.Copy),
             reads=[r_src], writes=[r_dst_rest])

    def process_tile(T, Tnext):
        P, NS, NT = T.P, T.NS, T.NT
        first_tile = (gtile[0] == 0)
        gtile[0] += 1
        r_nT_all = r_nT[0:NS]

        if T.sample:
            lgs = T.lg_slot
            S.dma("pool", "c_lg", lambda e: e.dma_start(out=X[:, lgs, :], in_=ln_g.to_broadcast([128, 1024])), writes=[r_X[lgs]])
            S.dma("pool", "c_lb2", lambda e: e.dma_start(out=cB[:, :], in_=ln_b.to_broadcast([128, 1024])), writes=r_relu)

        todo = [s_ for s_ in range(NS) if s_ not in T.a_done]
        stg = [phaseA_stages(T, s_) for s_ in todo]
        n = len(todo)
        a_pending = {}
        for k in range(n + 1):
            if k < n:
                stg[k][0]()
            if k >= 1:
                stg[k - 1][1]()
                if T.a_done:
                    a_pending[todo[k - 1]] = stg[k - 1][2]
                else:
                    stg[k - 1][2]()

        pend_pe = []

        def tok_block(slots, epis):
            for s in range(NS):
                if s in a_pending:
                    a_pending.pop(s)()
                bl = []
                for slot in slots:
                    b = alloc1()
                    bl.append(b)
                    for kc in range(8):
                        S.op("pe", lambda e, kc=kc, b=b, s=s, slot=slot: e.matmul(ps[0:P, b, 0:512], lhsT=nT[:, kc, s * P:(s + 1) * P],
                                                                                rhs=ring[:, slot, kc, :], start=(kc == 0), stop=(kc == 7)),
                             reads=[r_nT[s], r_ring[slot]], writes=[r_bank[b]], inc=(kc == 7))
                if len(pend_pe) >= 1:
                    pend_pe.pop(0)()
                for epi, b in zip(epis, bl):
                    epi(s, b)

        def feat_block(slot, rhs_buf, r_rhs, epi, fc0):
            for fcl in range(4):
                b = alloc1()
                for kc in range(8):
                    S.op("pe", lambda e, kc=kc, b=b, fcl=fcl: e.matmul(ps[:, b, 0:NT], lhsT=ring[:, slot, kc, fcl * 128:(fcl + 1) * 128],
                                                                     rhs=rhs_buf[:, kc, 0:NT], start=(kc == 0), stop=(kc == 7)),
                         reads=r_rhs + [r_ring[slot]], writes=[r_bank[b]], inc=(kc == 7))
                epi(fc0 + fcl, b)

        def v_epi_half(half):
            def epi(s, b):
                S.op("act", lambda e: e.activation(out=gv[0:P, half * 512:(half + 1) * 512], in_=ps[0:P, b, 0:512], func=AF.Gelu),
                     reads=[r_bank[b]], writes=[r_gv[half]])
                if half == 1:
                    S.op("dve", lambda e: e.bn_stats(out=stat[0:P, 0:6], in_=gv[0:P, 0:512]), reads=[r_gv[0]], writes=[rstat("bn0")])
                    S.op("dve", lambda e: e.bn_stats(out=stat[0:P, 6:12], in_=gv[0:P, 512:1024]), reads=[r_gv[1]], writes=[rstat("bn1")])
                    S.op("dve", lambda e: e.bn_aggr(out=stat[0:P, 12:14], in_=stat[0:P, 0:12]),
                         reads=[rstat("bn0"), rstat("bn1")], writes=[rstat("mv")])
                    tkey = rstat("lnt")
                    S.op("dve", lambda e: e.tensor_scalar(out=stat[0:P, 14:15], in0=stat[0:P, 13:14], scalar1=EPS, scalar2=None, op0=ALU.add),
                         reads=[rstat("mv")], writes=[tkey])
                    S.op("pool", lambda e: e.tensor_tensor(out=stat[0:P, 15:16], in0=stat[0:P, 14:15], in1=mhalf[0:P, 0:1], op=ALU.pow),
                         reads=[tkey, rstat("mhalf")], writes=[rstat("lnrs")])
                    S.op("dve", lambda e: e.scalar_tensor_tensor(out=stat[0:P, 40:41], in0=stat[0:P, 12:13], scalar=-1.0, in1=stat[0:P, 15:16],
                                                                 op0=ALU.mult, op1=ALU.mult),
                         reads=[rstat("mv"), rstat("lnrs")], writes=[rstat("lnnb")])
                    if T.sample:
                        S.op("act", lambda e: e.activation(out=tmpF[0:P, :], in_=gv[0:P, :], func=AF.Identity,
                                                           bias=stat[0:P, 40:41], scale=stat[0:P, 15:16]),
                             reads=[r_gv[0], r_gv[1], rstat("lnrs"), rstat("lnnb")], writes=[r_tmpF])
                        S.op("dve", lambda e: e.tensor_copy(out=vn[0:P, s, :], in_=tmpF[0:P, :]), reads=[r_tmpF], writes=[r_vn[s]])
                        S.op("dve", lambda e: e.tensor_tensor(out=gv[0:P, :], in0=tmpF[0:P, :], in1=X[0:P, T.lg_slot, :], op=ALU.mult),
                             reads=[r_tmpF, r_X[T.lg_slot]], writes=[r_gv[0], r_gv[1]])
                        S.op("dve", lambda e: e.tensor_tensor(out=gv[0:P, :], in0=gv[0:P, :], in1=cB[0:P, :], op=ALU.add),
                             reads=[r_gv[0], r_gv[1]] + r_relu, writes=[r_gv[0], r_gv[1]])
                        S.dma("pool", "o_vsg", lambda e: e.dma_start(out=vsg[s, :, :], in_=gv[0:P, :]),
                              reads=[r_gv[0], r_gv[1]])
                    else:
                        S.op("act", lambda e: e.activation(out=vn[0:P, s, :], in_=gv[0:P, :], func=AF.Identity,
                                                           bias=stat[0:P, 40:41], scale=stat[0:P, 15:16]),
                             reads=[r_gv[0], r_gv[1], rstat("lnrs"), rstat("lnnb")], writes=[r_vn[s]])
            return epi

        if first_tile:
            pass
        sl0 = get_block(0)
        sl1 = get_block(1)
        tok_block([sl0, sl1], [v_epi_half(0), v_epi_half(1)])


        def q_epi_half(half):
            def epi(s, b):
                qb = s % 3
                src3 = ps[0:P, b, 0:512].rearrange("p (h d) -> p h d", d=64)
                dst3 = qtok[0:P, qb, half * 512:(half + 1) * 512].rearrange("p (h d) -> p h d", d=64)
                rope_ops(T, s, src3, 8, dst3, r_bank[b], r_qtok[qb][0], r_qtok[qb][1], 0)
                if half == 1:
                    def pe_part(s=s, qb=qb):
                        bb = alloc1()
                        for hp in range(8):
                            S.op("pe", lambda e, hp=hp, bb=bb: e.transpose(tr_view(bb)[:, hp, 0:P], qtok[0:P, qb, hp * 128:(hp + 1) * 128],
                                                                         ident[0:P, 0:P]),
                                 reads=[r_qtok[qb][0], r_qtok[qb][1], r_const], writes=[r_bank[bb]], inc=(hp == 7))
                        S.op("act", lambda e, bb=bb: e.activation(out=QT[:, :, s * P:(s + 1) * P], in_=tr_view(bb)[:, :, 0:P], func=AF.Copy),
                             reads=[r_bank[bb]], writes=[r_QT])
                    pend_pe.append(pe_part)
            return epi

        sl0 = get_block(2)
        sl1 = get_block(3)
        tok_block([sl0, sl1], [q_epi_half(0), q_epi_half(1)])

        def kv_epi(s, b):
            ks = T.kslot_new(s)
            nk = P
            src3 = ps[0:P, b, 0:256].rearrange("p (h d) -> p h d", d=64)
            dst3 = kf[0:P, :].rearrange("p (h d) -> p h d", d=64)
            rope_ops(T, s, src3, 4, dst3, r_bank[b], r_kf[0], r_kf[1], 1)
            kf3 = kf[0:P, :].rearrange("p (h d) -> p h d", d=64)
            S.op("dve", lambda e: e.tensor_copy(out=kb[0:P, s, :, 0:64], in_=kf3), reads=[r_kf[0], r_kf[1]], writes=[r_kb[s]])
            S.op("dve", lambda e: e.tensor_copy(out=kb[0:P, s, :, 64:128], in_=kf3), reads=[r_kf[0], r_kf[1]], writes=[r_kb[s]])
            S.op("act", lambda e: e.activation(out=Vx[0:P, ks, :, 0:64], in_=ps[0:P, b, 256:512].rearrange("p (h d) -> p h d", d=64),
                                               func=AF.Copy),
                 reads=[r_bank[b]], writes=[r_V[ks]])
            st = T.state_out(s)
            if st is not None:
                kdst, vdst, key = st
                S.op("act", lambda e: e.activation(out=vf[0:P, :], in_=ps[0:P, b, 256:512], func=AF.Copy),
                     reads=[r_bank[b]], writes=[r_vf])
                S.dma("pool", key + "k", lambda e: e.dma_start(out=kdst, in_=kf[0:P, :]), reads=[r_kf[0], r_kf[1]])
                S.dma("pool", key + "v", lambda e: e.dma_start(out=vdst, in_=vf[0:P, :]), reads=[r_vf])
                out_sems.extend([key + "k", key + "v"])

            def pe_part(s=s, ks=ks):
                bb = alloc1()
                for g in range(4):
                    S.op("pe", lambda e, g=g, bb=bb: e.transpose(tr_view(bb)[:, g, 0:P],
                                                               kb[0:P, s, g, :], ident[0:P, 0:P]),
                         reads=[r_kb[s], r_const], writes=[r_bank[bb]], inc=(g == 3))
                S.op("dve", lambda e, bb=bb: e.tensor_copy(out=Ke[0:64, :, ks * 128:ks * 128 + P], in_=tr_view(bb)[0:64, 0:4, 0:P]),
                     reads=[r_bank[bb]], writes=[r_K[ks]])
                S.op("act", lambda e, bb=bb: e.activation(out=Ko[64:128, :, ks * 128:ks * 128 + P], in_=tr_view(bb)[64:128, 0:4, 0:P], func=AF.Copy),
                     reads=[r_bank[bb]], writes=[r_K[ks]])
            pend_pe.append(pe_part)

        sl = get_block(4)
        tok_block([sl], [kv_epi])

        if T.sample:
            for s in range(NS):
                ka = T.kslot_cache(s)
                S.dma("pool", "ck_ld", lambda e, s=s: e.dma_start(out=tmpF[:, 0:256], in_=ck[s, :, :]), writes=[r_tmpF])
                r_kbc = Res("kbc")
                j3 = junk[:, 0:512].rearrange("p (h d) -> p h d", d=128)
                t3 = tmpF[:, 0:256].rearrange("p (h d) -> p h d", d=64)
                S.op("dve", lambda e, j3=j3, t3=t3: e.tensor_copy(out=j3[:, :, 0:64], in_=t3), reads=[r_tmpF], writes=[r_junk])
                S.op("dve", lambda e, j3=j3, t3=t3: e.tensor_copy(out=j3[:, :, 64:128], in_=t3), reads=[r_tmpF], writes=[r_junk])
                S.dma("pool", "cv_ld", lambda e, s=s: e.dma_start(out=tmpF[:, 256:512], in_=cv[s, :, :]), writes=[r_tmpF])
                S.op("act", lambda e, ka=ka: e.activation(out=Vx[:, ka, :, 0:64], in_=tmpF[:, 256:512].rearrange("p (h d) -> p h d", d=64),
                                                          func=AF.Copy),
                     reads=[r_tmpF], writes=[r_V[ka]])
                bb = alloc1()
                for g in range(4):
                    S.op("pe", lambda e, g=g, bb=bb: e.transpose(tr_view(bb)[:, g, :], junk[:, g * 128:(g + 1) * 128], ident[:, :]),
                         reads=[r_junk, r_const], writes=[r_bank[bb]], inc=(g == 3))
                S.op("dve", lambda e, bb=bb, ka=ka: e.tensor_copy(out=Ke[0:64, :, ka * 128:ka * 128 + 128], in_=tr_view(bb)[0:64, 0:4, :]),
                     reads=[r_bank[bb]], writes=[r_K[ka]])
                S.op("dve", lambda e, bb=bb, ka=ka: e.tensor_copy(out=Ko[64:128, :, ka * 128:ka * 128 + 128], in_=tr_view(bb)[64:128, 0:4, :]),
                     reads=[r_bank[bb]], writes=[r_K[ka]])

        def u_epi(fc, b):
            S.op("act", lambda e: e.activation(out=guT[:, fc, 0:NT], in_=ps[:, b, 0:NT], func=AF.Gelu),
                 reads=[r_bank[b]], writes=[r_gu])

        def gate_epi(which):
            def epi(fc, b):
                S.op("act", lambda e: e.activation(out=tg[:, which, fc, 0:NT], in_=ps[:, b, 0:NT], func=AF.Tanh, scale=0.5),
                     reads=[r_bank[b]], writes=[r_tg[which]])
            return epi

        fb = [(5, u_epi, 0), (6, u_epi, 4)]
        gate_blocks = [(7, gate_epi(0), 0), (8, gate_epi(0), 4), (9, gate_epi(1), 0), (10, gate_epi(1), 4)]
        for bi, epi, fc0 in fb:
            for _ in range(2):
                if pend_pe:
                    pend_pe.pop(0)()
            sl = get_block(bi)
            feat_block(sl, nT, r_nT_all, epi, fc0)
        while pend_pe:
            pend_pe.pop(0)()

        if first_tile:
            prepass(range(11, 17))

        for s in range(NS):
            b0 = alloc2()
            for g in range(8):
                S.op("pe", lambda e, g=g, b0=b0, s=s: e.matmul(ps[:, b0 + g // 4, (g % 4) * 128:(g % 4) * 128 + P],
                                                             lhsT=vn[0:P, s, g * 128:(g + 1) * 128], rhs=WsT[0:P, g, 0:P],
                                                             start=True, stop=True),
                     reads=[r_vn[s], rstat("WsT")], writes=[r_bank[b0], r_bank[b0 + 1]], inc=(g == 7))
            psv = ps[:, b0:b0 + 2, :].rearrange("p a (g i) -> p (a g) i", i=128)[:, :, 0:P]
            tS = tmpF[:, :].rearrange("p (g i) -> p g i", i=128)[:, :, 0:P]
            btv = BTs[:, :, 0:P] if T.sample else BT[:, :, 0:P]
            btk = "BTs" if T.sample else "BT"
            S.op("dve", lambda e, psv=psv, tS=tS: e.tensor_tensor(out=tS, in0=psv, in1=bc(lngT, 0, 128, 8, [[1, 8], [0, P]]), op=ALU.mult),
                 reads=[r_bank[b0], r_bank[b0 + 1], rstat("c_lgT")], writes=[r_tmpF])
            S.op("dve", lambda e, tS=tS, btv=btv: e.tensor_tensor(out=tS, in0=tS, in1=btv, op=ALU.add),
                 reads=[r_tmpF, rstat(btk)], writes=[r_tmpF])
            S.op("dve", lambda e, tS=tS, s=s: e.tensor_tensor(out=aT[:, :, s * P:(s + 1) * P], in0=tS, in1=guT[:, :, s * P:(s + 1) * P], op=ALU.mult),
                 reads=[r_tmpF, r_gu], writes=[r_aT])

        NC4 = 4 * P
        bt_pending = []
        gate_ctr = [0]
        gate_slot = [None]
        for s in range(NS):
            kts = T.keytiles(s)
            obase = 0
            sbase = 4
            ob = s % 2
            pv_pending = []
            for g in range(4):
                pb = g % 2
                for (xi, slot, nk, regions) in kts:
                    b = sbase + xi
                    for par, Kt in enumerate((Ke, Ko)):
                        S.op("pe", lambda e, b=b, par=par, Kt=Kt, slot=slot, nk=nk, g=g, s=s: e.matmul(
                            ps[0:nk, b, par * 2 * P:(par + 1) * 2 * P].rearrange("p (a q) -> p a q", q=P),
                            lhsT=Kt[:, g, slot * 128:slot * 128 + nk],
                            rhs=QT[:, 2 * g:2 * g + 2, s * P:(s + 1) * P], start=True, stop=True),
                            reads=[r_K[slot], r_QT], writes=[r_bank[b]], inc=(par == 1))
                    for (r0, r1, q0, q1) in regions:
                        S.op("act", lambda e, b=b, xi=xi, r0=r0, r1=r1, q0=q0, q1=q1, pb=pb: e.activation(
                            out=PT[r0:r1, pb, xi, 0:NC4].rearrange("p (a q) -> p a q", q=P)[:, :, q0:q1],
                            in_=ps[r0:r1, b, 0:NC4].rearrange("p (a q) -> p a q", q=P)[:, :, q0:q1],
                            func=AF.Exp, scale=0.125),
                            reads=[r_bank[b]], writes=[r_PT[pb][xi]])
                for _ in range(4 // NS):
                    gci = gate_ctr[0]
                    gate_ctr[0] += 1
                    bi, epi, fc0 = gate_blocks[gci // 4]
                    fcl = gci % 4
                    if fcl == 0:
                        gate_slot[0] = get_block(bi)
                    gsl = gate_slot[0]
                    gb_ = sbase + 2 + (gci % 2)
                    for kc in range(8):
                        S.op("pe", lambda e, kc=kc, gb_=gb_, fcl=fcl, gsl=gsl: e.matmul(ps[:, gb_, 0:NT], lhsT=ring[:, gsl, kc, fcl * 128:(fcl + 1) * 128],
                                                                                   rhs=nT[:, kc, 0:NT], start=(kc == 0), stop=(kc == 7)),
                             reads=r_nT_all + [r_ring[gsl]], writes=[r_bank[gb_]], inc=(kc == 7))
                    epi(fc0 + fcl, gb_)
                if g == 1 and bt_pending:
                    bt_pending.pop(0)()

                def pv(g=g, pb=pb, kts=kts, s=s, obase=obase):
                    bo = obase + g
                    for sl in range(4):
                        par, pl = sl // 2, sl % 2
                        j = 2 * pl + par
                        for ki, (xi, slot, nk, regions) in enumerate(kts):
                            last = (ki == len(kts) - 1)
                            S.op("pe", lambda e, bo=bo, xi=xi, slot=slot, nk=nk, ki=ki, last=last, sl=sl, j=j: e.matmul(
                                ps[0:P, bo, j * 65:j * 65 + 65], lhsT=PT[0:nk, pb, xi, sl * P:(sl + 1) * P], rhs=Vx[0:nk, slot, g, 0:65],
                                start=(ki == 0), stop=last),
                                reads=[r_V[slot], r_PT[pb][xi]], writes=[r_bank[bo]], inc=(last and sl == 3))
                if pv_pending:
                    pv_pending.pop(0)()
                pv_pending.append(pv)
            while pv_pending:
                pv_pending.pop(0)()
            obanks = [r_bank[obase + i] for i in range(4)]
            dv = bass.AP(ps, obase * 512 + 64, [[4096, P], [512, 4], [65, 4]])
            ov = bass.AP(ps, obase * 512, [[4096, P], [512, 4], [65, 4], [1, 64]])
            ds_ = dstat[0:P, ob * 32:ob * 32 + 16]
            rr_ = dstat[0:P, ob * 32 + 16:ob * 32 + 32]
            S.op("dve", lambda e, dv=dv, ds_=ds_: e.tensor_tensor(out=ds_.rearrange("p (a b) -> p a b", b=4), in0=dv,
                                                                 in1=esB[0:P, :].rearrange("p (a b) -> p a b", b=4), op=ALU.add),
                 reads=obanks + [rstat("esB")], writes=[r_dstat[ob]])
            S.op("dve", lambda e, ds_=ds_, rr_=rr_: e.reciprocal(out=rr_, in_=ds_), reads=[r_dstat[ob]], writes=[r_dstat[ob]])
            S.op("dve", lambda e, ov=ov, rr_=rr_, ob=ob: e.tensor_tensor(
                out=btok[0:P, ob, :].rearrange("p (a b d) -> p a b d", b=4, d=64), in0=ov,
                in1=bc(dstat, ob * 32 + 16, P, 64, [[4, 4], [1, 4], [0, 64]]), op=ALU.mult),
                reads=obanks + [r_dstat[ob]], writes=[r_btok[ob]])

            def bt_part(s=s, ob=ob, obase=obase):
                tb_ = obase + 3
                for hp in range(8):
                    S.op("pe", lambda e, hp=hp: e.transpose(tr_view(tb_)[:, hp, 0:P], btok[0:P, ob, hp * 128:(hp + 1) * 128], ident[0:P, 0:P]),
                         reads=[r_btok[ob], r_const], writes=[r_bank[tb_]], inc=(hp == 7))
                S.op("dve", lambda e: e.tensor_copy(out=bT[:, :, s * P:(s + 1) * P], in_=tr_view(tb_)[:, :, 0:P]),
                     reads=[r_bank[tb_]], writes=[r_gu])
            bt_pending.append(bt_part)
        while bt_pending:
            bt_pending.pop(0)()
        bank_ptr[0] = 0

        if first_tile:
            prepass(range(17, 25))

        for half in range(2):
            sla = get_block(11 + 2 * half)
            slb = get_block(12 + 2 * half)
            bas, bbs = [], []
            for fcl in range(4):
                ba = alloc1()
                bas.append(ba)
                for kc in range(8):
                    S.op("pe", lambda e, kc=kc, ba=ba, fcl=fcl, sla=sla: e.matmul(ps[:, ba, 0:NT], lhsT=ring[:, sla, kc, fcl * 128:(fcl + 1) * 128],
                                                                               rhs=aT[:, kc, 0:NT], start=(kc == 0), stop=(kc == 7)),
                         reads=[r_aT, r_ring[sla]], writes=[r_bank[ba]], inc=(kc == 7))
            def gate_a(fcl, half=half, bas=bas):
                fc = half * 4 + fcl
                ba = bas[fcl]
                tb = fc % 2
                S.op("dve", lambda e, fc=fc, ba=ba, tb=tb: e.scalar_tensor_tensor(out=t1[:, tb, 0:NT], in0=tg[:, 0, fc, 0:NT], scalar=1.0,
                                                                                 in1=ps[:, ba, 0:NT], op0=ALU.add, op1=ALU.mult),
                     reads=[r_tg[0], r_bank[ba]], writes=[r_t1[tb]])
            gate_a(0)
            gate_a(1)
            for fcl in range(4):
                bbk = alloc1()
                bbs.append(bbk)
                for kc in range(8):
                    S.op("pe", lambda e, kc=kc, bbk=bbk, fcl=fcl, slb=slb: e.matmul(ps[:, bbk, 0:NT], lhsT=ring[:, slb, kc, fcl * 128:(fcl + 1) * 128],
                                                                                 rhs=bT[:, kc, 0:NT], start=(kc == 0), stop=(kc == 7)),
                         reads=[r_gu, r_ring[slb]], writes=[r_bank[bbk]], inc=(kc == 7))
                fc = half * 4 + fcl
                tb = fc % 2
                S.op("dve", lambda e, fc=fc, bbk=bbk, tb=tb: e.scalar_tensor_tensor(out=relu_t[:, tb, 0:NT], in0=tg[:, 1, fc, 0:NT], scalar=1.0,
                                                                                   in1=ps[:, bbk, 0:NT], op0=ALU.add, op1=ALU.mult),
                     reads=[r_tg[1], r_bank[bbk]], writes=[r_relu[tb]])
                S.op("dve", lambda e, fc=fc, tb=tb: e.tensor_tensor(out=mT[:, fc, 0:NT], in0=t1[:, tb, 0:NT], in1=relu_t[:, tb, 0:NT], op=ALU.add),
                     reads=[r_t1[tb], r_relu[tb]], writes=[r_QT])
                if fcl + 2 < 4:
                    gate_a(fcl + 2)

        if first_tile:
            prepass(range(25, 33))

        slo = [get_block(15), get_block(16)]
        fb0 = {}

        def f_st1(s):
            b0 = alloc2()
            fb0[s] = b0
            for cb in range(2):
                for kc in range(8):
                    S.op("pe", lambda e, kc=kc, cb=cb, b0=b0, s=s: e.matmul(ps[0:P, b0 + cb, 0:512], lhsT=mT[:, kc, s * P:(s + 1) * P],
                                                                          rhs=ring[:, slo[cb], kc, :], start=(kc == 0), stop=(kc == 7)),
                         reads=[r_QT, r_ring[slo[cb]]], writes=[r_bank[b0 + cb]], inc=(kc == 7))

        def f_st1_epi(s):
            b0 = fb0[s]
            pz = ps[0:P, b0:b0 + 2, :].rearrange("p a c -> p (a c)")
            c_ssq, c_rs = 24 + s, 28 + s
            S.op("act", lambda e, pz=pz, c_ssq=c_ssq: e.activation(out=junk[0:P, :], in_=pz, func=AF.Square, scale=0.5,
                                                                  accum_out=stat[0:P, c_ssq:c_ssq + 1]),
                 reads=[r_bank[b0], r_bank[b0 + 1]], writes=[r_junk, rstat("ssq%d" % c_ssq)])
            rsqrt_chain(stat[0:P, c_ssq:c_ssq + 1], stat[0:P, c_rs:c_rs + 1], P, 1, 1.0 / D,
                        rstat("ssq%d" % c_ssq), rstat("rs%d" % c_rs), "rc%d" % c_rs)

        def f_st2(s):
            xsl = T.xslots[s]
            b0 = fb0[s]
            pz = ps[0:P, b0:b0 + 2, :].rearrange("p a c -> p (a c)")
            c_rs = 28 + s
            tbuf, tres = (tmpF, [r_tmpF]) if s % 2 == 0 else (gv, [r_gv[0], r_gv[1]])
            S.op("dve", lambda e, pz=pz, c_rs=c_rs, tbuf=tbuf: e.scalar_tensor_tensor(out=tbuf[0:P, :], in0=pz, scalar=stat[0:P, c_rs:c_rs + 1], in1=g2B[0:P, :],
                                                                                     op0=ALU.mult, op1=ALU.mult),
                 reads=[r_bank[b0], r_bank[b0 + 1], rstat("rs%d" % c_rs), rstat("g2B")], writes=tres)
            S.op("dve", lambda e, xsl=xsl, tbuf=tbuf: e.tensor_tensor(out=X[0:P, xsl, :], in0=X[0:P, xsl, :], in1=tbuf[0:P, :], op=ALU.add),
                 reads=tres + [r_X[xsl]], writes=[r_X[xsl]])

        fstg = [norm_stages(T, s_, T.xslots[s_], g3T, "c_g3T", 32 + s_, 36 + s_) for s_ in range(NS)]
        for k in range(NS + 3):
            if k < NS:
                f_st1(k)
            if 0 <= k - 1 < NS:
                f_st2(k - 1)
            if k < NS:
                f_st1_epi(k)
            if 0 <= k - 3 < NS:
                fstg[k - 3][2]()
            if 0 <= k - 1 < NS:
                fstg[k - 1][0]()
            if 0 <= k - 2 < NS:
                fstg[k - 2][1]()

        def h_epi(fc, b):
            tb = fc % 2
            S.op("act", lambda e: e.activation(out=relu_t[:, tb, 0:NT], in_=ps[:, b, 0:NT], func=AF.Relu),
                 reads=[r_bank[b]], writes=[r_relu[tb]])
            S.op("dve", lambda e: e.tensor_tensor(out=hT[:, fc, 0:NT], in0=relu_t[:, tb, 0:NT], in1=relu_t[:, tb, 0:NT], op=ALU.mult),
                 reads=[r_relu[tb]], writes=[r_hT, r_gu, r_QT, r_aT] + r_vn)

        hoist = []
        for j in range(8):
            if j == 5 and Tnext is not None:
                hs = [s_ for s_ in range(Tnext.NS)]
                hoist = [phaseA_stages(Tnext, s_) for s_ in hs]
                Tnext.a_done = set(hs)
                for st in hoist:
                    st[0]()
                for st in hoist[0:3]:
                    st[1]()
            sl = get_block(17 + j)
            feat_block(sl, nT, r_nT_all, h_epi, j * 4)

        hbanks = [[None] * NS for _ in range(2)]
        for st in hoist[0:3]:
            st[2]()
        for st in hoist[3:]:
            st[1]()
            st[2]()
        for ch in range(2):
            for s in range(NS):
                hbanks[ch][s] = alloc1()
            for pc in range(4):
                sl = get_block(25 + ch * 4 + pc)
                for s in range(NS):
                    b = hbanks[ch][s]
                    for kcl in range(8):
                        kc = pc * 8 + kcl
                        S.op("pe", lambda e, kc=kc, kcl=kcl, b=b, s=s, sl=sl: e.matmul(ps[0:P, b, 0:512], lhsT=hT[:, kc, s * P:(s + 1) * P],
                                                                                     rhs=ring[:, sl, kcl, :], start=(kc == 0), stop=(kc == 31)),
                             reads=[r_hT, r_gu, r_QT, r_aT] + r_vn + [r_ring[sl]], writes=[r_bank[b]], inc=(kcl == 7))
            for s in range(NS):
                b = hbanks[ch][s]
                col = 44 + ch
                S.op("act", lambda e, b=b, s=s, ch=ch: e.activation(out=junk[0:P, 0:512], in_=ps[0:P, b, 0:512], func=AF.Square,
                                                                  accum_out=stat[0:P, 52 + ch * 4 + s:53 + ch * 4 + s]),
                     reads=[r_bank[b]], writes=[r_junk, rstat("hss%d_%d" % (ch, s))])
        for s in range(NS):
            S.op("dve", lambda e, s=s: e.tensor_tensor(out=stat[0:P, 52 + s:53 + s], in0=stat[0:P, 52 + s:53 + s], in1=stat[0:P, 56 + s:57 + s], op=ALU.add),
                 reads=[rstat("hss0_%d" % s), rstat("hss1_%d" % s)], writes=[rstat("hss0_%d" % s)])
            rsqrt_chain(stat[0:P, 52 + s:53 + s], stat[0:P, 56 + s:57 + s], P, 1, 1.0 / D,
                        rstat("hss0_%d" % s), rstat("hss1_%d" % s), "hrc%d" % s)
        def h_scale(s):
            tbuf, tres = (tmpF, [r_tmpF]) if s % 2 == 0 else (gv, [r_gv[0], r_gv[1]])
            for ch in range(2):
                b = hbanks[ch][s]
                S.op("dve", lambda e, b=b, s=s, ch=ch, tbuf=tbuf: e.scalar_tensor_tensor(out=tbuf[0:P, ch * 512:(ch + 1) * 512], in0=ps[0:P, b, 0:512],
                                                                                        scalar=stat[0:P, 56 + s:57 + s], in1=g4B[0:P, ch * 512:(ch + 1) * 512],
                                                                                        op0=ALU.mult, op1=ALU.mult),
                     reads=[r_bank[b], rstat("hss1_%d" % s), rstat("g4B")], writes=tres)

        def h_add(s):
            xsl = T.xslots[s]
            tbuf, tres = (tmpF, [r_tmpF]) if s % 2 == 0 else (gv, [r_gv[0], r_gv[1]])
            S.op("dve", lambda e, xsl=xsl, tbuf=tbuf: e.tensor_tensor(out=X[0:P, xsl, :], in0=X[0:P, xsl, :], in1=tbuf[0:P, :], op=ALU.add),
                 reads=tres + [r_X[xsl]], writes=[r_X[xsl]])
            dst = T.y_dst(s)
            S.dma("pool", "xst%d" % xsl, lambda e, dst=dst, xsl=xsl: e.dma_start(out=dst, in_=X[0:P, xsl, :]), reads=[r_X[xsl]])
            if ("xst%d" % xsl) not in out_sems:
                out_sems.append("xst%d" % xsl)

        for s in range(NS + 1):
            if s < NS:
                h_scale(s)
            if s >= 1:
                h_add(s - 1)

    def load_x(T, s_list):
        for s in s_list:
            xsl = T.xslots[s]
            src = T.x_src(s)
            S.dma("pool", "xld%d" % xsl, lambda e, src=src, xsl=xsl, P=T.P: e.dma_start(out=X[0:P, xsl, :], in_=src), writes=[r_X[xsl]])

    tiles = []
    for ti in range(n_ptiles):
        T = Tile()
        T.a_done = set()
        T.sample = False
        T.P, T.NS, T.NT = 128, 4, 512
        seq, tt = ti // 8, ti % 8
        T.seq, T.tt = seq, tt
        T.xslots = [(xslot_ctr[0] + s) % NXS for s in range(4)]
        xslot_ctr[0] += 4
        T.rope_blk = (lambda s, tt=tt: tt * 4 + s)
        T.x_src = (lambda s, seq=seq, tt=tt: xp[seq, tt * 512 + s * 128: tt * 512 + (s + 1) * 128, :])
        T.y_dst = (lambda s, seq=seq, tt=tt: yp[seq, tt * 512 + s * 128: tt * 512 + (s + 1) * 128, :])
        T.kslot_new = (lambda s, tt=tt: (tt * 4 + s) % NKS)

        def keytiles(s, tt=tt):
            gs = tt * 4 + s
            res = []
            if gs > 0:
                res.append((0, (gs - 1) % NKS, 128, [(0, 128, 0, 64), (64, 128, 64, 128)]))
            res.append((1, gs % NKS, 128, [(0, 64, 0, 64), (0, 128, 64, 128)]))
            return res
        T.keytiles = keytiles

        def state_out(s, seq=seq, tt=tt):
            if tt == 7 and s == 3:
                return (kp[seq, :, :], vp[seq, :, :], "o_p%d" % seq)
            return None
        T.state_out = state_out
        tiles.append(T)

    if do_sample:
        T = Tile()
        T.a_done = set()
        T.sample = True
        T.P, T.NS, T.NT = 16, 2, 32
        T.xslots = [(xslot_ctr[0] + s) % NXS for s in range(2)]
        T.lg_slot = (xslot_ctr[0] + 2) % NXS
        xslot_ctr[0] += 2
        T.rope_blk = (lambda s: 32)
        T.x_src = (lambda s: xsm[s, :, :])
        T.y_dst = (lambda s: ys[s, :, :])
        T.kslot_new = (lambda s: 2 * s + 1)
        T.kslot_cache = (lambda s: 2 * s)
        T.keytiles = (lambda s: [(0, 2 * s, 128, [(0, 128, 0, 16)]), (1, 2 * s + 1, 16, [(0, 16, 0, 16)])])
        T.state_out = (lambda s: (ksn[s, :, :], vsn[s, :, :], "o_s%d" % s))
        tiles.append(T)

    for i, T in enumerate(tiles):
        if i == 0:
            load_x(T, range(T.NS))
        if i + 1 < len(tiles):
            load_x(tiles[i + 1], range(tiles[i + 1].NS))
        process_tile(T, tiles[i + 1] if i + 1 < len(tiles) else None)

    if do_sample:
        out_sems.append("o_vsg")
    S.wait_all("pool", [k for k in dict.fromkeys(out_sems)])

    stuck = S.check_deadlock()
    if stuck:
        raise RuntimeError("static deadlock detected: %r" % (stuck,))
    with nc.Block() as block:
        @block.tensor
        def _(e):
            for f in S.prog["pe"]:
                f(e)

        @block.scalar
        def _(e):
            for f in S.prog["act"]:
                f(e)

        @block.vector
        def _(e):
            for f in S.prog["dve"]:
                f(e)

        @block.gpsimd
        def _(e):
            for f in S.prog["pool"]:
                f(e)

        @block.sync
        def _(e):
            for f in S.prog["sp"]:
                f(e)
    return nc


def _rope_table():
    half = 8
    inv = (np.float32(500000.0) ** (-np.arange(half, dtype=np.float32) * np.float32(2.0) / np.float32(16))).astype(np.float32)
    tab = np.zeros((128, 33, 32), np.float32)
    p = np.arange(128)
    for blk in range(33):
        if blk < 32:
            pos = (blk * 128 + p).astype(np.float32)
        else:
            pos = (4096 + (p % 16)).astype(np.float32)
        ang = (pos[:, None] * inv[None, :]).astype(np.float32)
        c = np.cos(ang).astype(np.float32)
        s = np.sin(ang).astype(np.float32)
        tab[:, blk, 0:8] = c
        tab[:, blk, 8:16] = c
        tab[:, blk, 16:24] = -s
        tab[:, blk, 24:32] = s
    return tab


_NC_CACHE = {}


def kernel(x_prompt, x_sample, cache_swa_k, cache_swa_v, w_in, sgu_ln_g, sgu_ln_b, sgu_w, sgu_b, attn_sinks,
           w_branch_a, w_branch_b, w_out, g_mix_pre, g_mix_post, g_ffn_pre, g_ffn_post, w_ff1, w_ff2,
           _n_ptiles=16, _do_sample=True):
    f = lambda a: np.ascontiguousarray(np.asarray(a, dtype=np.float32))
    x_prompt = np.asarray(x_prompt, dtype=np.float32)
    x_sample = f(x_sample)
    ckf = f(cache_swa_k)[0].reshape(16, 128, 256)
    cvf = f(cache_swa_v)[0].reshape(16, 128, 256)
    shared = {
        "w_in": f(w_in)[0], "wa": f(w_branch_a)[0], "wb": f(w_branch_b)[0], "wo": f(w_out)[0],
        "w1": f(w_ff1)[0], "w2": f(w_ff2)[0], "sgu_w": f(sgu_w)[0], "sgu_b": f(sgu_b)[0].reshape(1, 1024),
        "ln_g": f(sgu_ln_g)[0].reshape(1, 1024), "ln_b": f(sgu_ln_b)[0].reshape(1, 1024),
        "sinks": f(attn_sinks)[0].reshape(1, 16),
        "g1": f(g_mix_pre)[0].reshape(1, 1024), "g2": f(g_mix_post)[0].reshape(1, 1024),
        "g3": f(g_ffn_pre)[0].reshape(1, 1024), "g4": f(g_ffn_post)[0].reshape(1, 1024),
        "rope": _rope_table(), "ident": np.eye(128, dtype=np.float32),
    }
    in_maps = []
    for c in range(NCORES):
        m = dict(shared)
        m["xp"] = np.ascontiguousarray(x_prompt[2 * c:2 * c + 2])
        m["xs"] = np.ascontiguousarray(x_sample[2 * c:2 * c + 2])
        m["ck"] = np.ascontiguousarray(ckf[2 * c:2 * c + 2])
        m["cv"] = np.ascontiguousarray(cvf[2 * c:2 * c + 2])
        in_maps.append(m)
    key = (_n_ptiles, _do_sample)
    if key not in _NC_CACHE:
        _NC_CACHE[key] = build_program(_n_ptiles, _do_sample)
    nc = _NC_CACHE[key]
    res = run_bass_kernel_spmd(nc, in_maps, core_ids=list(range(NCORES)))
    R = res.results
    y_prompt = np.concatenate([r["yp"] for r in R], axis=0)
    y_sample = np.concatenate([r["ys"] for r in R], axis=0)
    kpo = np.concatenate([r["kp"] for r in R], axis=0).reshape(1, 16, 128, 4, 64)
    vpo = np.concatenate([r["vp"] for r in R], axis=0).reshape(1, 16, 128, 4, 64)
    kso = np.concatenate([r["ksn"] for r in R], axis=0).reshape(1, 16, 16, 4, 64)
    vso = np.concatenate([r["vsn"] for r in R], axis=0).reshape(1, 16, 16, 4, 64)
    vsgo = np.concatenate([r["vsg"] for r in R], axis=0).reshape(1, 16, 16, 1024)
    return (y_prompt, y_sample, kpo, vpo, kso, vso, vsgo)
```
